# Optimizing a Trainium2 kernel written in Bass

```python
import math
import jax, jax.numpy as jnp
from jax import lax
import numpy as np

D_MODEL = 1024
BATCH = 16
SEQ = 2048
DEPTH = 2

N_EVEN = (DEPTH + 1) // 2
N_ODD = DEPTH // 2
EPS = 1e-6
Q_BLOCK = 128

POOL_WINDOWS = (2, 4, 8, 16)
POOL_GROUPS = len(POOL_WINDOWS)
POOL_WIDTH = D_MODEL // 2
POOL_GROUP_DIM = POOL_WIDTH // POOL_GROUPS

MLA_HEADS = 8
MLA_NOPE_DIM = 64
MLA_ROPE_DIM = 32
MLA_QK_DIM = MLA_NOPE_DIM + MLA_ROPE_DIM
MLA_V_DIM = 64
MLA_Q_RANK = 256
MLA_KV_RANK = 256
ROPE_THETA = 10000.0
MLA_WIDTH = MLA_HEADS * MLA_V_DIM
HYB_IN = POOL_WIDTH + MLA_Q_RANK + MLA_KV_RANK + MLA_ROPE_DIM
HYB_OUT = POOL_WIDTH + MLA_WIDTH

DIFF_HEAD_DIM = 64
DIFF_HEADS = D_MODEL // (2 * DIFF_HEAD_DIM)
DIFF_V_DIM = 2 * DIFF_HEAD_DIM
DIFF_WIDTH = DIFF_HEADS * DIFF_V_DIM
DIFF_QK_WIDTH = DIFF_HEADS * 2 * DIFF_HEAD_DIM

D_FF = ((8 * D_MODEL + 3 * 256 - 1) // (3 * 256)) * 256

kernel_name = "hybrid_pool_mla_diffattn_encoder"


def rmsnorm(x, g):
    xf = x.astype(jnp.float32)
    y = xf * lax.rsqrt(jnp.mean(xf * xf, axis=-1, keepdims=True) + EPS)
    return (y * g.astype(jnp.float32)).astype(x.dtype)


def rope_tables(seq, dim):
    inv = ROPE_THETA ** (-jnp.arange(0, dim, 2, dtype=jnp.float32) / dim)
    ang = jnp.arange(seq, dtype=jnp.float32)[:, None] * inv[None, :]
    return jnp.cos(ang), jnp.sin(ang)


def apply_rope(x, cos, sin):
    xf = x.astype(jnp.float32)
    half = x.shape[-1] // 2
    x1, x2 = xf[..., :half], xf[..., half:]
    out = jnp.concatenate([x1 * cos - x2 * sin, x2 * cos + x1 * sin], axis=-1)
    return out.astype(x.dtype)


def alibi_slopes(n_heads):
    return jnp.asarray(2.0 ** (-8.0 * np.arange(1, n_heads + 1) / n_heads), dtype=jnp.float32)


def multi_scale_pool(u):
    B, S, _ = u.shape
    uf = u.astype(jnp.float32).reshape(B, S, POOL_GROUPS, POOL_GROUP_DIM)
    cs = jnp.concatenate([jnp.zeros((B, 1, POOL_GROUPS, POOL_GROUP_DIM), jnp.float32),
                          jnp.cumsum(uf, axis=1)], axis=1)
    t = jnp.arange(S)
    outs = []
    for g, w in enumerate(POOL_WINDOWS):
        lo = jnp.clip(t - w // 2, 0, S)
        hi = jnp.clip(t + w // 2, 0, S)
        csg = cs[:, :, g, :]
        win_sum = csg[:, hi] - csg[:, lo]
        cnt = (hi - lo).astype(jnp.float32)[None, :, None]
        outs.append(win_sum / cnt - uf[:, :, g, :])
    return jnp.stack(outs, axis=2).astype(u.dtype)


def mla_attention(q, k, v):
    B, S, H, Dq = q.shape
    Dv = v.shape[-1]
    nb = S // Q_BLOCK
    qb = q.reshape(B, nb, Q_BLOCK, H, Dq).transpose(1, 0, 2, 3, 4)
    scale = Dq ** -0.5

    def block(qblk):
        s = jnp.einsum('bqhd,bkhd->bhqk', qblk, k).astype(jnp.float32) * scale
        p = jax.nn.softmax(s, axis=-1).astype(v.dtype)
        return jnp.einsum('bhqk,bkhd->bqhd', p, v)

    o = lax.map(block, qb)
    return o.transpose(1, 0, 2, 3, 4).reshape(B, S, H * Dv)


def diff_attention(q, k, v, lam, slopes):
    B, S, H, _, d = q.shape
    nb = S // Q_BLOCK
    qb = q.reshape(B, nb, Q_BLOCK, H, 2, d).transpose(1, 0, 2, 3, 4, 5)
    starts = jnp.arange(nb) * Q_BLOCK
    kpos = jnp.arange(S)
    scale = d ** -0.5

    def block(args):
        qblk, start = args
        qpos = start + jnp.arange(Q_BLOCK)
        dist = jnp.abs(qpos[:, None] - kpos[None, :]).astype(jnp.float32)
        bias = -slopes[:, None, None] * dist[None]
        s = jnp.einsum('bqhmd,bkhmd->mbhqk', qblk, k).astype(jnp.float32) * scale + bias[None, None]
        p = jax.nn.softmax(s, axis=-1)
        a = (p[0] - lam * p[1]).astype(v.dtype)
        return jnp.einsum('bhqk,bkhe->bqhe', a, v)

    o = lax.map(block, (qb, starts))
    return o.transpose(1, 0, 2, 3, 4).reshape(B, S, H, 2 * d)


def hybrid_pool_mla(hn, w_in, pool_w, pool_scale, q_norm_g, w_uq, kv_norm_g, w_ukv, w_out, cos, sin):
    B, S, _ = hn.shape
    proj = hn @ w_in
    o1 = POOL_WIDTH
    o2 = o1 + MLA_Q_RANK
    o3 = o2 + MLA_KV_RANK
    u_pool, c_q, c_kv, k_r = proj[..., :o1], proj[..., o1:o2], proj[..., o2:o3], proj[..., o3:]

    pooled = multi_scale_pool(u_pool)
    a_out = jnp.einsum('bsgc,gcd->bsgd', pooled, pool_w).reshape(B, S, POOL_WIDTH) * pool_scale

    q = (rmsnorm(c_q, q_norm_g) @ w_uq).reshape(B, S, MLA_HEADS, MLA_QK_DIM)
    q_nope, q_rope = q[..., :MLA_NOPE_DIM], q[..., MLA_NOPE_DIM:]
    q_rope = apply_rope(q_rope, cos[None, :, None, :], sin[None, :, None, :])
    kv = (rmsnorm(c_kv, kv_norm_g) @ w_ukv).reshape(B, S, MLA_HEADS, MLA_NOPE_DIM + MLA_V_DIM)
    k_nope, v = kv[..., :MLA_NOPE_DIM], kv[..., MLA_NOPE_DIM:]
    k_rope = apply_rope(k_r, cos[None], sin[None])
    k = jnp.concatenate([k_nope, jnp.broadcast_to(k_rope[:, :, None, :], (B, S, MLA_HEADS, MLA_ROPE_DIM))], axis=-1)
    q = jnp.concatenate([q_nope, q_rope], axis=-1)
    b_out = mla_attention(q, k, v)

    return jnp.concatenate([a_out, b_out], axis=-1) @ w_out


def diff_block(hn, w_qkv, lq1, lk1, lq2, lk2, subln_g, w_out, lam_init, slopes):
    B, S, _ = hn.shape
    qkv = hn @ w_qkv
    q = qkv[..., :DIFF_QK_WIDTH].reshape(B, S, DIFF_HEADS, 2, DIFF_HEAD_DIM)
    k = qkv[..., DIFF_QK_WIDTH:2 * DIFF_QK_WIDTH].reshape(B, S, DIFF_HEADS, 2, DIFF_HEAD_DIM)
    v = qkv[..., 2 * DIFF_QK_WIDTH:].reshape(B, S, DIFF_HEADS, DIFF_V_DIM)
    lam = (jnp.exp(jnp.sum(lq1.astype(jnp.float32) * lk1.astype(jnp.float32)))
           - jnp.exp(jnp.sum(lq2.astype(jnp.float32) * lk2.astype(jnp.float32))) + lam_init)
    o = diff_attention(q, k, v, lam, slopes)
    o = rmsnorm(o, subln_g) * (1.0 - lam_init)
    return o.reshape(B, S, DIFF_WIDTH) @ w_out


def swiglu(h, w_gate, w_up, w_down):
    return (jax.nn.silu(h @ w_gate) * (h @ w_up)) @ w_down


def setup_inputs(seed: int = 0) -> dict:
    key = jax.random.key(seed)
    ks = jax.random.split(key, 24)
    f32 = jnp.float32

    def nrm(k, shape, fan_in):
        return jax.random.normal(k, shape, f32) * (fan_in ** -0.5)

    def gain(k, shape):
        return 1.0 + 0.05 * jax.random.normal(k, shape, f32)

    return {
        "x": jax.random.normal(ks[0], (BATCH, SEQ, D_MODEL), f32),
        "attn_norm_g": gain(ks[1], (DEPTH, D_MODEL)),
        "ffn_norm_g": gain(ks[2], (DEPTH, D_MODEL)),
        "final_norm_g": gain(ks[3], (D_MODEL,)),
        "hyb_w_in": nrm(ks[4], (N_EVEN, D_MODEL, HYB_IN), D_MODEL),
        "pool_w": nrm(ks[5], (N_EVEN, POOL_GROUPS, POOL_GROUP_DIM, POOL_GROUP_DIM), POOL_GROUP_DIM),
        "pool_scale": 1.0 + 0.1 * jax.random.normal(ks[6], (N_EVEN, POOL_WIDTH), f32),
        "mla_q_norm_g": gain(ks[7], (N_EVEN, MLA_Q_RANK)),
        "mla_w_uq": nrm(ks[8], (N_EVEN, MLA_Q_RANK, MLA_HEADS * MLA_QK_DIM), MLA_Q_RANK),
        "mla_kv_norm_g": gain(ks[9], (N_EVEN, MLA_KV_RANK)),
        "mla_w_ukv": nrm(ks[10], (N_EVEN, MLA_KV_RANK, MLA_HEADS * (MLA_NOPE_DIM + MLA_V_DIM)), MLA_KV_RANK),
        "hyb_w_out": nrm(ks[11], (N_EVEN, HYB_OUT, D_MODEL), HYB_OUT),
        "diff_w_qkv": nrm(ks[12], (N_ODD, D_MODEL, 2 * DIFF_QK_WIDTH + DIFF_WIDTH), D_MODEL),
        "diff_lambda_q1": 0.1 * jax.random.normal(ks[13], (N_ODD, DIFF_HEAD_DIM), f32),
        "diff_lambda_k1": 0.1 * jax.random.normal(ks[14], (N_ODD, DIFF_HEAD_DIM), f32),
        "diff_lambda_q2": 0.1 * jax.random.normal(ks[15], (N_ODD, DIFF_HEAD_DIM), f32),
        "diff_lambda_k2": 0.1 * jax.random.normal(ks[16], (N_ODD, DIFF_HEAD_DIM), f32),
        "diff_subln_g": gain(ks[17], (N_ODD, DIFF_V_DIM)),
        "diff_w_out": nrm(ks[18], (N_ODD, DIFF_WIDTH, D_MODEL), DIFF_WIDTH),
        "ffn_w_gate": nrm(ks[19], (DEPTH, D_MODEL, D_FF), D_MODEL),
        "ffn_w_up": nrm(ks[20], (DEPTH, D_MODEL, D_FF), D_MODEL),
        "ffn_w_down": nrm(ks[21], (DEPTH, D_FF, D_MODEL), D_FF),
    }


def reference(x, attn_norm_g, ffn_norm_g, final_norm_g, hyb_w_in, pool_w, pool_scale,
              mla_q_norm_g, mla_w_uq, mla_kv_norm_g, mla_w_ukv, hyb_w_out,
              diff_w_qkv, diff_lambda_q1, diff_lambda_k1, diff_lambda_q2, diff_lambda_k2,
              diff_subln_g, diff_w_out, ffn_w_gate, ffn_w_up, ffn_w_down):
    S = x.shape[1]
    cos, sin = rope_tables(S, MLA_ROPE_DIM)
    slopes = alibi_slopes(DIFF_HEADS)
    h = x
    for layer in range(DEPTH):
        hn = rmsnorm(h, attn_norm_g[layer])
        i = layer // 2
        if layer % 2 == 0:
            mix = hybrid_pool_mla(hn, hyb_w_in[i], pool_w[i], pool_scale[i], mla_q_norm_g[i],
                                  mla_w_uq[i], mla_kv_norm_g[i], mla_w_ukv[i], hyb_w_out[i], cos, sin)
        else:
            lam_init = 0.8 - 0.6 * math.exp(-0.3 * layer)
            mix = diff_block(hn, diff_w_qkv[i], diff_lambda_q1[i], diff_lambda_k1[i],
                             diff_lambda_q2[i], diff_lambda_k2[i], diff_subln_g[i], diff_w_out[i],
                             lam_init, slopes)
        h = h + mix
        h = h + swiglu(rmsnorm(h, ffn_norm_g[layer]), ffn_w_gate[layer], ffn_w_up[layer], ffn_w_down[layer])
    return rmsnorm(h, final_norm_g)
```

```python
import math
import os
import numpy as np
import concourse.bass as bass
import concourse.mybir as mybir
from concourse.bass_utils import run_bass_kernel_spmd
from contextlib import ExitStack

F32 = mybir.dt.float32
BF16 = mybir.dt.bfloat16
AF = mybir.ActivationFunctionType
ALU = mybir.AluOpType

S = 2048
D = 1024
NCH = 8
DFF = 2816
NF = 22
EPS = 1e-6
NCORES = 8
SLOT = 2048
NSLOT = 4
ARENA_KB = 207


def _tile_kc(w, cols):
    K = w.shape[0]
    sub = w[:, cols].reshape(K // 128, 128, len(cols))
    return np.ascontiguousarray(sub.transpose(1, 0, 2)).reshape(128, -1)


def weight_catalog(inp=None):
    items = []

    def add(key, F, fn):
        items.append((key, F, fn))

    ar = np.arange
    for o in range(4, 8):
        add(("win", o), 8 * 128, lambda o=o: _tile_kc(inp["hyb_w_in"][0], ar(o * 128, (o + 1) * 128)))
    for half in range(2):
        add(("wpool", half), 8 * 256, lambda half=half: _tile_kc(inp["hyb_w_in"][0], ar(half * 256, (half + 1) * 256)))
    swap = (ar(32) + 16) % 32
    add(("wkr", 0), 8 * 96, lambda: _tile_kc(inp["hyb_w_in"][0], np.concatenate([ar(64), 1024 + ar(32)])))
    add(("wkr", 1), 8 * 96, lambda: _tile_kc(inp["hyb_w_in"][0], np.concatenate([ar(64), 1024 + swap])))
    add(("poolw",), 4 * 128, lambda: np.ascontiguousarray(inp["pool_w"][0].transpose(1, 0, 2)).reshape(128, -1))
    add(("wv",), 2 * 512, lambda: _tile_kc(inp["mla_w_ukv"][0], (ar(512) // 64) * 128 + 64 + ar(512) % 64))
    for h in range(8):
        add(("wq", h), 2 * 2 * 96 + 2 * 64, lambda h=h: np.concatenate([
            _tile_kc(inp["mla_w_uq"][0], h * 96 + ar(96)),
            _tile_kc(inp["mla_w_uq"][0], h * 96 + np.concatenate([ar(64), 64 + swap])),
            _tile_kc(inp["mla_w_ukv"][0], h * 128 + ar(64))], axis=1))
    for n in range(8):
        add(("wo0", n), 8 * 128, lambda n=n: _tile_kc(inp["hyb_w_out"][0], ar(n * 128, (n + 1) * 128)))
    for l in range(2):
        for half in range(2):
            for f in range(NF):
                add(("wg", l, half, f), 8 * 256, lambda l=l, f=f: np.concatenate([
                    _tile_kc(inp["ffn_w_gate"][l], ar(f * 128, (f + 1) * 128)),
                    _tile_kc(inp["ffn_w_up"][l], ar(f * 128, (f + 1) * 128))], axis=1))
            for n in range(8):
                for fh in range(2):
                    add(("wd", l, half, n, fh), 11 * 128, lambda l=l, n=n, fh=fh: _tile_kc(
                        inp["ffn_w_down"][l][fh * 1408:(fh + 1) * 1408], ar(n * 128, (n + 1) * 128)))
        if l == 0:
            for h in range(8):
                add(("wqk", h), 8 * 256, lambda h=h: _tile_kc(
                    inp["diff_w_qkv"][0], np.concatenate([h * 128 + ar(128), 1024 + h * 128 + ar(128)])))
                add(("wvv", h), 8 * 128, lambda h=h: _tile_kc(inp["diff_w_qkv"][0], 2048 + h * 128 + ar(128)))
            for g2 in range(4):
                for n in range(8):
                    add(("wo1h", g2, n), 2 * 128, lambda n=n, g2=g2: _tile_kc(inp["diff_w_out"][0][g2 * 256:(g2 + 1) * 256], ar(n * 128, (n + 1) * 128)))
    cat = {}
    off = 0
    arrays = []
    seen = {}
    for key, F, fn in items:
        base = key
        if key[0] in ("wg", "wd"):
            base = (key[0], key[1]) + tuple(key[3:])
        if base in seen:
            cat[key] = seen[base]
            continue
        cat[key] = (off, F)
        seen[base] = (off, F)
        if inp is not None:
            a = fn()
            assert a.shape == (128, F), (key, a.shape, F)
            arrays.append(a.astype(np.float32, copy=False))
        off += F
    flat = np.concatenate(arrays, axis=1) if inp is not None else None
    return cat, off, flat


CONST_LAYOUT = {}


def const_catalog(inp=None):
    items = []

    def add(name, n, fn):
        items.append((name, n, fn))

    def fm(v):
        return np.ascontiguousarray(np.asarray(v).reshape(-1, 128).T)

    add("attn_g0", 8, lambda: fm(inp["attn_norm_g"][0]))
    add("attn_g1", 8, lambda: fm(inp["attn_norm_g"][1]))
    add("ffn_g0", 8, lambda: fm(inp["ffn_norm_g"][0]))
    add("ffn_g1", 8, lambda: fm(inp["ffn_norm_g"][1]))
    add("final_g", 8, lambda: fm(inp["final_norm_g"]))
    add("qn_g", 2, lambda: fm(inp["mla_q_norm_g"][0]))
    add("kvn_g", 2, lambda: fm(inp["mla_kv_norm_g"][0]))
    add("pool_scale", 4, lambda: fm(inp["pool_scale"][0]))
    add("subln_g", 128, lambda: np.broadcast_to(inp["diff_subln_g"][0][None, :], (128, 128)))
    for nm in ("diff_lambda_q1", "diff_lambda_k1", "diff_lambda_q2", "diff_lambda_k2"):
        add(nm, 64, lambda nm=nm: np.broadcast_to(inp[nm][0][None, :], (128, 64)))
    add("ident", 128, lambda: np.eye(128, dtype=np.float32))
    slopes = 2.0 ** (-np.arange(1, 9, dtype=np.float64))

    def kb():
        p = np.arange(128)[:, None, None, None]
        j = np.arange(16)[None, None, None, :]
        sg = np.array([1.0, -1.0])[None, None, :, None]
        m = slopes[None, :, None, None]
        return (sg * m * (128 * j + p - 1024.0)).reshape(128, -1).astype(np.float32)
    add("kb", 8 * 2 * 16, kb)
    pos = np.arange(S, dtype=np.float32)
    inv = (10000.0 ** (-np.arange(0, 32, 2, dtype=np.float32) / 32)).astype(np.float32)
    ang = pos[None, :] * inv[:, None]
    r = np.arange(128) % 32
    add("cos", S, lambda: np.cos(ang)[r % 16].astype(np.float32))
    add("sins", S, lambda: (np.sin(ang)[r % 16] * np.where(r < 16, -1.0, 1.0)[:, None]).astype(np.float32))
    q_ = np.arange(S)

    def qext():
        t = np.zeros((128, S), np.float32)
        t[64] = 128.0 * (q_ // 128)
        t[65] = q_ % 128
        t[66] = 1.0
        t[67] = 1.0
        return t

    def kext():
        t = np.zeros((128, S), np.float32)
        t[64] = -1.0
        t[65] = -1.0
        t[66] = 128.0 * (q_ // 128)
        t[67] = q_ % 128
        return t
    add("qext", S, qext)
    add("kext", S, kext)
    slopes8 = 2.0 ** (-np.arange(1, 9, dtype=np.float64))
    add("cdiag", 1024, lambda: (2.0 * slopes8[None, :, None] * np.minimum(np.arange(128)[None, None, :] - np.arange(128)[:, None, None], 0.0)
                                ).reshape(128, 1024).astype(np.float32))

    def band():
        out = np.zeros((128, 4, 5, 128), np.float32)
        for g, w in enumerate((2, 4, 8, 16)):
            for var, tb in enumerate((5, 0, 15)):
                t = np.arange(tb * 128, (tb + 1) * 128)
                lo = np.clip(t - w // 2, 0, S)
                hi = np.clip(t + w // 2, 0, S)
                cnt = (hi - lo).astype(np.float32)
                sidx = np.arange(tb * 128, (tb + 1) * 128)
                m = ((sidx[:, None] >= lo[None, :]) & (sidx[:, None] < hi[None, :])).astype(np.float32) / cnt[None, :]
                m -= np.eye(128, dtype=np.float32)
                out[:, g, var, :] = m
            tb = 5
            t = np.arange(tb * 128, (tb + 1) * 128)
            lo = t - w // 2
            hi = t + w // 2
            sprev = np.arange((tb - 1) * 128, tb * 128)
            snext = np.arange((tb + 1) * 128, (tb + 2) * 128)
            out[:, g, 3, :] = ((sprev[:, None] >= lo[None, :]) & (sprev[:, None] < hi[None, :])).astype(np.float32) / w
            out[:, g, 4, :] = ((snext[:, None] >= lo[None, :]) & (snext[:, None] < hi[None, :])).astype(np.float32) / w
        return out.reshape(128, -1)
    add("band", 4 * 5 * 128, band)

    lay = {}
    off = 0
    arrays = []
    for name, n, fn in items:
        lay[name] = (off, n)
        if inp is not None:
            a = np.asarray(fn(), dtype=np.float32)
            assert a.shape == (128, n), (name, a.shape)
            arrays.append(a)
        off += n
    flat = np.ascontiguousarray(np.concatenate(arrays, axis=1)) if inp is not None else None
    return lay, off, flat


class Buf:
    __slots__ = ("name", "w", "r")

    def __init__(self, name=""):
        self.name = name
        self.w = None
        self.r = {}


class Eng:
    def __init__(self, name, sem):
        self.name = name
        self.sem = sem
        self.count = 0
        self.waited = {}
        self.prog = []
        self.snaps = {}


class DSem:
    def __init__(self, h):
        self.h = h
        self.val = 0


class Tracker:
    def __init__(self):
        self.dry = False
        self.eng = {}
        self.dsems = {"sp": [], "pool": []}
        self.dnext = {"sp": 0, "pool": 0}
        self.semsnap = {}
        self.nops = 0

    def _deps(self, reads, writes):
        deps = {}

        def add(t):
            if t is None:
                return
            k = id(t[0])
            if k not in deps or deps[k][1] < t[1]:
                deps[k] = t
        for b in reads:
            add(b.w)
        for b in writes:
            add(b.w)
            for t in b.r.values():
                add(t)
        return deps

    def _emit_waits(self, E, deps):
        for k, (sem, val) in sorted(deps.items(), key=lambda kv: -kv[1][1]):
            if sem is E.sem:
                if E.name == "pe":
                    continue
                assert val <= E.count
            else:
                pe = self.eng["pe"]
                if sem is pe.sem and val > pe.count:
                    raise RuntimeError("wait on pending PE ticket from %s (deadlock risk)" % E.name)
            if E.waited.get(k, 0) >= val:
                continue
            E.prog.append(("wait", sem, val))
            E.waited[k] = val
            snap = self.semsnap.get(k, {}).get(val)
            if snap:
                for k2, v2 in snap.items():
                    if E.waited.get(k2, 0) < v2:
                        E.waited[k2] = v2

    def _mark(self, t, reads, writes):
        for b in writes:
            b.w = t
            b.r = {}
        k = id(t[0])
        for b in reads:
            if k not in b.r or b.r[k][1] < t[1]:
                b.r[k] = t

    def op(self, ename, fn, reads=(), writes=(), inc=True):
        if self.dry:
            return
        E = self.eng[ename]
        self._emit_waits(E, self._deps(reads, writes))
        E.prog.append(("op", fn, inc))
        self.nops += 1
        if inc:
            E.count += 1
            t = (E.sem, E.count)
            self.semsnap.setdefault(id(E.sem), {})[E.count] = dict(E.waited)
        else:
            assert ename == "pe"
            t = (E.sem, E.count + 1)
        self._mark(t, reads, writes)

    def dma(self, ename, fn, reads=(), writes=()):
        if self.dry:
            return None
        E = self.eng[ename]
        pool = self.dsems[ename]
        ds = pool[self.dnext[ename] % len(pool)]
        self.dnext[ename] += 1
        deps = self._deps(reads, writes)
        if ds.val > 0:
            deps[id(ds.h)] = (ds.h, ds.val)
        self._emit_waits(E, deps)
        E.prog.append(("dma", fn, ds.h))
        ds.val += 16
        t = (ds.h, ds.val)
        self.semsnap.setdefault(id(ds.h), {})[ds.val] = dict(E.waited)
        self._mark(t, reads, writes)
        return t

    def wait_ticket(self, ename, t):
        if self.dry or t is None:
            return
        E = self.eng[ename]
        self._emit_waits(E, {id(t[0]): t})


class Arena:
    def __init__(self, tensor, nbytes):
        self.t = tensor
        self.n = nbytes
        self.live = []
        self.dead = []

    def alloc(self, nbytes, dtype, nbufs=1, name=""):
        nbytes = (nbytes + 63) // 64 * 64
        cands = [0] + sorted(o + s for o, s, _ in self.live)
        off = None
        for c in cands:
            if c + nbytes > self.n:
                continue
            if all(c + nbytes <= o or c >= o + s for o, s, _ in self.live):
                off = c
                break
        if off is None:
            raise RuntimeError("arena full allocating %s (%d B); live=%s" % (name, nbytes, [(o, s) for o, s, _ in self.live]))
        bufs = [Buf("%s%d" % (name, i)) for i in range(nbufs)]
        newdead = []
        for o, s, bs in self.dead:
            if o < off + nbytes and off < o + s:
                for ob in bs:
                    for nb in bufs:
                        if ob.w is not None:
                            k = id(ob.w[0])
                            if k not in nb.r or nb.r[k][1] < ob.w[1]:
                                nb.r[k] = ob.w
                        for k, t in ob.r.items():
                            if k not in nb.r or nb.r[k][1] < t[1]:
                                nb.r[k] = t
                if not (o >= off and o + s <= off + nbytes):
                    newdead.append((o, s, bs))
            else:
                newdead.append((o, s, bs))
        self.dead = newdead
        self.live.append((off, nbytes, bufs))
        if not hasattr(self, "log"):
            self.log = {}
        self.log[name] = (off, nbytes)
        ap = self.t[:, off // 2:(off + nbytes) // 2]
        if dtype == F32:
            ap = ap.bitcast(F32)
        return Region(self, off, nbytes, ap, bufs)

    def free(self, reg):
        for i, (o, s, bs) in enumerate(self.live):
            if o == reg.off:
                self.dead.append(self.live.pop(i))
                return
        raise RuntimeError("double free")


class Region:
    def __init__(self, arena, off, nbytes, ap, bufs):
        self.arena = arena
        self.off = off
        self.nbytes = nbytes
        self.ap = ap
        self.bufs = bufs

    def free(self):
        self.arena.free(self)


class Program:
    def __init__(self, nseq, stop_after=None):
        self.nseq = nseq
        self.stop_after = stop_after
        self.wcat, self.wtot, _ = weight_catalog(None)
        self.clay, self.ctot, _ = const_catalog(None)
        self.wsched = []

    def bank(self, group="g"):
        lst = self.bank_groups[group]
        i = self.bank_ctr.get(group, 0)
        self.bank_ctr[group] = i + 1
        b = lst[i % len(lst)]
        return b

    def PS(self, b, lo=0, hi=512, p0=0, p1=128):
        return self.ps[p0:p1, b * 512 + lo:b * 512 + hi]

    def wget(self, key, hold=0):
        T = self.T
        if T.dry:
            self.wsched.append(key)
            off, F = self.wcat[key]
            return self.wring.ap[:, 0:F], self.wring.bufs[0]
        idx = self.wuse
        assert self.wsched[idx] == key, (self.wsched[idx], key)
        while self.wissued < min(len(self.wsched), idx + NSLOT - hold):
            k = self.wsched[self.wissued]
            slot = self.wissued % NSLOT
            off, F = self.wcat[k]
            assert F <= SLOT
            dst = self.wring.ap[:, slot * SLOT: slot * SLOT + F]
            src = self.wflat[:, off:off + F]
            T.dma("pool", lambda e, dst=dst, src=src: e.dma_start(out=dst, in_=src, max_dma_last_dim=8192),
                  writes=[self.wring.bufs[slot]])
            self.wissued += 1
        slot = idx % NSLOT
        self.wuse += 1
        off, F = self.wcat[key]
        return self.wring.ap[:, slot * SLOT: slot * SLOT + F], self.wring.bufs[slot]

    def cst(self, name):
        off, n = self.clay[name]
        return self.cpers[:, off:off + n]

    def mm(self, out, lhsT, rhs, start, stop, reads, writes, inc=None, **kw):
        if inc is None:
            inc = stop
        self.T.op("pe", lambda e: e.matmul(out, lhsT, rhs, start=start, stop=stop, **kw), reads=reads, writes=writes, inc=inc)

    def rmsnorm(self, gname, dst_fn, dst_bufs_fn, nfeat=D):
        T = self.T
        A = self.arena
        sq = A.alloc(2 * 1024, BF16, 2, "sq")
        rs = A.alloc(2 * 2048, F32, 2, "rstd")
        g = self.cst(gname)
        for tc in range(4):
            b = self.bank()
            for kc in range(NCH):
                sqa = sq.ap[:, (kc % 2) * 512:(kc % 2 + 1) * 512]
                src = self.Hap(kc, tc)
                T.op("act", lambda e, sqa=sqa, src=src: e.activation(out=sqa, in_=src, func=AF.Square),
                     reads=[self.Hb[kc][tc]], writes=[sq.bufs[kc % 2]])
                self.mm(self.PS(b), self.ones, sqa, kc == 0, kc == NCH - 1, [sq.bufs[kc % 2]], [self.psb[b]], inc=True)
            r = rs.ap[:, (tc % 2) * 512:(tc % 2 + 1) * 512]
            rb = rs.bufs[tc % 2]
            T.op("act", lambda e, r=r, b=b: e.activation(out=r, in_=self.PS(b), func=AF.Ln, scale=1.0 / nfeat, bias=self.epsb),
                 reads=[self.psb[b]], writes=[rb])
            T.op("act", lambda e, r=r: e.activation(out=r, in_=r, func=AF.Exp, scale=-0.5), reads=[rb], writes=[rb])
            for kc in range(NCH):
                o = dst_fn(kc, tc)
                src = self.Hap(kc, tc)
                T.op("dve", lambda e, o=o, src=src, kc=kc, r=r: e.scalar_tensor_tensor(
                    out=o, in0=src, scalar=g[:, kc:kc + 1], in1=r, op0=ALU.mult, op1=ALU.mult),
                    reads=[self.Hb[kc][tc], rb], writes=dst_bufs_fn(kc, tc))
        sq.free()
        rs.free()

    def Hap(self, kc, tc, lo=0, hi=512):
        return self.H.ap[:, kc * S + tc * 512 + lo: kc * S + tc * 512 + hi]

    def HNap(self, kc, t0, t1):
        return self.HN.ap[:, kc * S + t0: kc * S + t1]

    def resid_proj(self, wname, src_ap_fn, src_bufs_fn, nk):
        T = self.T
        for n in range(NCH):
            w, wb = self.wget((wname, n))
            for tc in range(4):
                b = self.bank()
                for kc in range(nk):
                    self.mm(self.PS(b), w[:, kc * 128:(kc + 1) * 128], src_ap_fn(kc, tc), kc == 0, kc == nk - 1,
                            [wb] + src_bufs_fn(kc, tc), [self.psb[b]])
                h = self.Hap(n, tc)
                T.op("dve", lambda e, h=h, b=b: e.tensor_tensor(out=h, in0=h, in1=self.PS(b), op=ALU.add),
                     reads=[self.psb[b]], writes=[self.Hb[n][tc]])

    def ffn(self, l):
        T = self.T
        A = self.arena
        self.rmsnorm("ffn_g%d" % l, lambda kc, tc: self.HNap(kc, tc * 512, (tc + 1) * 512), lambda kc, tc: [self.HNb[kc][tc]])
        gated = A.alloc(NF * 1024 * 2, BF16, NF * 2, "gated")
        sil = A.alloc(2 * 2048, F32, 2, "sil")
        for half in range(2):
            for f in range(NF):
                w, wb = self.wget(("wg", l, half, f))
                for t2 in range(2):
                    tc = half * 2 + t2
                    bg = self.bank()
                    bu = self.bank()
                    for which, b in ((0, bg), (1, bu)):
                        for kc in range(NCH):
                            self.mm(self.PS(b), w[:, which * 1024 + kc * 128: which * 1024 + (kc + 1) * 128],
                                    self.HNap(kc, tc * 512, (tc + 1) * 512), kc == 0, kc == NCH - 1,
                                    [wb, self.HNb[kc][tc]], [self.psb[b]])
                    k = (f * 2 + t2) % 2
                    sa = sil.ap[:, k * 512:(k + 1) * 512]
                    T.op("act", lambda e, sa=sa, bg=bg: e.activation(out=sa, in_=self.PS(bg), func=AF.Silu),
                         reads=[self.psb[bg]], writes=[sil.bufs[k]])
                    ga = gated.ap[:, f * 1024 + t2 * 512: f * 1024 + (t2 + 1) * 512]
                    T.op("dve", lambda e, ga=ga, sa=sa, bu=bu: e.tensor_tensor(out=ga, in0=sa, in1=self.PS(bu), op=ALU.mult),
                         reads=[sil.bufs[k], self.psb[bu]], writes=[gated.bufs[f * 2 + t2]])
            for n in range(NCH):
                w0, wb0 = self.wget(("wd", l, half, n, 0))
                w1, wb1 = self.wget(("wd", l, half, n, 1), hold=1)
                for t2 in range(2):
                    tc = half * 2 + t2
                    b = self.bank()
                    for f in range(NF):
                        w, wb = (w0, wb0) if f < 11 else (w1, wb1)
                        ff = f % 11
                        self.mm(self.PS(b), w[:, ff * 128:(ff + 1) * 128], gated.ap[:, f * 1024 + t2 * 512: f * 1024 + (t2 + 1) * 512],
                                f == 0, f == NF - 1, [wb, gated.bufs[f * 2 + t2]], [self.psb[b]])
                    h = self.Hap(n, tc)
                    T.op("dve", lambda e, h=h, b=b: e.tensor_tensor(out=h, in0=h, in1=self.PS(b), op=ALU.add),
                         reads=[self.psb[b]], writes=[self.Hb[n][tc]])
        gated.free()
        sil.free()

    def load_const(self, name, dtype=F32):
        off, n = self.clay[name]
        reg = self.arena.alloc(n * (4 if dtype == F32 else 2), dtype, 1, name)
        src = self.cflat[:, off:off + n]
        if dtype == F32:
            self.T.dma("sp", lambda e: e.dma_start(out=reg.ap, in_=src), writes=reg.bufs)
        else:
            self.T.dma("pool", lambda e: e.dma_start(out=reg.ap, in_=src, max_dma_last_dim=8192), writes=reg.bufs)
        return reg

    def attention_core(self, nmaps, kdim, vdim, qk_fn, vt_fn, scale, bias_fn, acc_done_fn, keep_fn=None, depth=3, side_units=None):
        T = self.T
        W1 = vdim + 1
        per_bank = 512 // W1
        nacc = nmaps * 4
        nbanks = (nacc + per_bank - 1) // per_bank
        per_pass = 4
        nbanks = (per_pass + per_bank - 1) // per_bank
        tiles = []
        for c in range(4):
            js = [j for j in range(16) if keep_fn is None or keep_fn(c, j)]
            for m in range(nmaps):
                for j in js:
                    tiles.append((c, j, m, j == js[0], j == js[-1]))
        inflight = {}
        chunk = {}
        deferred = []
        defer = 6

        def stage1(t):
            c, j, m, first, lastj = tiles[t]
            sb = self.bank("s")
            qk_fn(m, j, c, sb)
            bias = bias_fn(m, j, c, sb)
            pslot = self.ptctr % self.npt
            self.ptctr += 1
            pa = self.pt.ap[:, pslot * 512:(pslot + 1) * 512]
            pb = self.pt.bufs[pslot]
            if bias is None:
                T.op("act", lambda e, pa=pa, sb=sb: e.activation(out=pa, in_=self.PS(sb), func=AF.Exp, scale=scale),
                     reads=[self.psb[sb]], writes=[pb])
            else:
                bap, brd = bias
                T.op("act", lambda e, pa=pa, sb=sb, bap=bap: e.activation(out=pa, in_=self.PS(sb), func=AF.Exp, scale=scale, bias=bap),
                     reads=[self.psb[sb]] + brd, writes=[pb])
            inflight[t] = (pa, pb)

        def stage2(t):
            c, j, m, first, lastj = tiles[t]
            pa, pb = inflight.pop(t)
            if first:
                grp = "acc%d" % m if nmaps > 1 else "acc"
                abanks = [self.bank(grp) for _ in range(nbanks)]
                accs = [(abanks[qb // per_bank], (qb % per_bank) * W1) for qb in range(4)]
                chunk[(c, m)] = (accs, set())
            accs, started = chunk[(c, m)]
            va, vrd = vt_fn(m, j)
            for qb in range(4):
                ab, lo = accs[qb]
                st = ab not in started
                started.add(ab)
                self.mm(self.PS(ab, lo, lo + W1), pa[:, qb * 128:(qb + 1) * 128], va, st, lastj,
                        [pb] + vrd, [self.psb[ab]], inc=(qb == 3), skip_group_check=True)
            if lastj:
                later = acc_done_fn(c, m, accs)
                if later is not None:
                    deferred.append((t + depth + defer, later))
                del chunk[(c, m)]

        n = len(tiles)
        units = list(side_units or [])
        stride = max(1, n // (len(units) + 1)) if units else 1
        for t in range(n + depth):
            if t < n:
                stage1(t)
            if t >= depth:
                stage2(t - depth)
            if units and t >= 2 and (t - 2) % stride == 0:
                units.pop(0)()
            while deferred and deferred[0][0] <= t:
                deferred.pop(0)[1]()
        while units:
            units.pop(0)()
        while deferred:
            deferred.pop(0)[1]()

    def layer0(self):
        T = self.T
        A = self.arena
        HNf = lambda kc, tc: self.HNap(kc, tc * 512, (tc + 1) * 512)
        self.rmsnorm("attn_g0", HNf, lambda kc, tc: [self.HNb[kc][tc]])
        cqn = A.alloc(2 * S * 2, BF16, 8, "cqn")
        ckvn = A.alloc(2 * S * 2, BF16, 8, "ckvn")
        kr = A.alloc(S * 2, BF16, 4, "kr")
        cos = self.load_const("cos")
        sins = self.load_const("sins")
        band = self.load_const("band", BF16)
        utm = A.alloc(16 * 512 * 2, BF16, 16, "utm")
        tmp = A.alloc(3 * 2048, F32, 3, "tmp")
        sq = A.alloc(2 * 1024, BF16, 2, "sq0")

        for half in range(2):
            w, wb = self.wget(("wpool", half))
            for tb in range(16):
                b = self.bank()
                for kc in range(NCH):
                    self.mm(self.PS(b, 0, 256), self.HNap(kc, tb * 128, (tb + 1) * 128), w[:, kc * 256:(kc + 1) * 256], kc == 0, kc == NCH - 1,
                            [wb, self.HNb[kc][tb // 4]], [self.psb[b]])
                ua = utm.ap[:, tb * 512 + half * 256: tb * 512 + (half + 1) * 256]
                T.op("dve", lambda e, ua=ua, b=b: e.tensor_copy(out=ua, in_=self.PS(b, 0, 256)), reads=[self.psb[b]], writes=[utm.bufs[tb]])
        for (o0, gname, dst) in ((4, "qn_g", cqn), (6, "kvn_g", ckvn)):
            w0, wb0 = self.wget(("win", o0))
            w1, wb1 = self.wget(("win", o0 + 1), hold=1)
            g = self.cst(gname)
            for tc in range(4):
                bs = [self.bank(), self.bank()]
                for ci, (w_, wb_) in enumerate(((w0, wb0), (w1, wb1))):
                    for kc in range(NCH):
                        self.mm(self.PS(bs[ci]), w_[:, kc * 128:(kc + 1) * 128], HNf(kc, tc), kc == 0, kc == NCH - 1,
                                [wb_, self.HNb[kc][tc]], [self.psb[bs[ci]]])
                bm = self.bank()
                for ci in range(2):
                    sqa = sq.ap[:, ci * 512:(ci + 1) * 512]
                    T.op("act", lambda e, sqa=sqa, b=bs[ci]: e.activation(out=sqa, in_=self.PS(b), func=AF.Square),
                         reads=[self.psb[bs[ci]]], writes=[sq.bufs[ci]])
                    self.mm(self.PS(bm), self.ones, sqa, ci == 0, ci == 1, [sq.bufs[ci]], [self.psb[bm]], inc=True)
                r = tmp.ap[:, 0:512]
                T.op("act", lambda e, r=r, bm=bm: e.activation(out=r, in_=self.PS(bm), func=AF.Ln, scale=1.0 / 256, bias=self.epsb),
                     reads=[self.psb[bm]], writes=[tmp.bufs[0]])
                T.op("act", lambda e, r=r: e.activation(out=r, in_=r, func=AF.Exp, scale=-0.5), reads=[tmp.bufs[0]], writes=[tmp.bufs[0]])
                for ci in range(2):
                    o = dst.ap[:, ci * S + tc * 512: ci * S + (tc + 1) * 512]
                    T.op("dve", lambda e, o=o, b=bs[ci], ci=ci, r=r, g=g: e.scalar_tensor_tensor(
                        out=o, in0=self.PS(b), scalar=g[:, ci:ci + 1], in1=r, op0=ALU.mult, op1=ALU.mult),
                        reads=[self.psb[bs[ci]], tmp.bufs[0]], writes=[dst.bufs[ci * 4 + tc]])
        wa, wab = self.wget(("wkr", 0))
        wbb, wbbb = self.wget(("wkr", 1), hold=1)
        if True:
            for tc in range(4):
                ba, bb = self.bank(), self.bank()
                for (w_, wb_, b_) in ((wa, wab, ba), (wbb, wbbb, bb)):
                    for kc in range(NCH):
                        self.mm(self.PS(b_, 0, 512, 0, 96), w_[:, kc * 96:(kc + 1) * 96], HNf(kc, tc), kc == 0, kc == NCH - 1,
                                [wb_, self.HNb[kc][tc]], [self.psb[b_]])
                self.rope(ba, bb, cos, sins, tc, tmp, kr.ap[64:96, tc * 512:(tc + 1) * 512], [kr.bufs[tc]])
        cat = self.HN
        catb = self.HNb
        pw, pwb = self.wget(("poolw",))
        wv, wvb = self.wget(("wv",), hold=1)
        pooled = A.alloc(4 * 512 * 2, BF16, 4, "pooled")
        vt = A.alloc(16 * 8 * 65 * 2, BF16, 16, "vt")
        T.op("dve", lambda e: e.memset(vt.ap[:, 0:16 * 520].rearrange("p (a b) -> p a b", b=65)[:, :, 64], 1.0), writes=vt.bufs)
        psc = self.cst("pool_scale")
        items = [(g, tc) for g in range(4) for tc in range(4)]

        def band_part(i):
            g, tc = items[i]
            b = self.bank()
            for t4 in range(4):
                tb = tc * 4 + t4
                srcs = []
                if tb > 0:
                    srcs.append((tb - 1, 3))
                srcs.append((tb, 1 if tb == 0 else (2 if tb == 15 else 0)))
                if tb < 15:
                    srcs.append((tb + 1, 4))
                for k_, (sbk, var) in enumerate(srcs):
                    self.mm(self.PS(b, t4 * 128, (t4 + 1) * 128), utm.ap[:, sbk * 512 + g * 128: sbk * 512 + (g + 1) * 128],
                            band.ap[:, (g * 5 + var) * 128:(g * 5 + var + 1) * 128], (t4 == 0 and k_ == 0), k_ == len(srcs) - 1,
                            [utm.bufs[sbk]] + band.bufs, [self.psb[b]], inc=(k_ == len(srcs) - 1), skip_group_check=True)
            pa = pooled.ap[:, (i % 4) * 512:(i % 4 + 1) * 512]
            T.op("act", lambda e, pa=pa, b=b: e.activation(out=pa, in_=self.PS(b), func=AF.Copy), reads=[self.psb[b]], writes=[pooled.bufs[i % 4]])

        def lin_part(i):
            g, tc = items[i]
            pa = pooled.ap[:, (i % 4) * 512:(i % 4 + 1) * 512]
            b2 = self.bank()
            self.mm(self.PS(b2), pw[:, g * 128:(g + 1) * 128], pa, True, True, [pwb, pooled.bufs[i % 4]], [self.psb[b2]])
            o = self.HNap(g, tc * 512, (tc + 1) * 512)
            T.op("act", lambda e, o=o, b2=b2, g=g: e.activation(out=o, in_=self.PS(b2), func=AF.Identity, scale=psc[:, g:g + 1]),
                 reads=[self.psb[b2]], writes=[catb[g][tc]])

        def v_part(tb):
            b = self.bank()
            for kc in range(2):
                self.mm(self.PS(b), ckvn.ap[:, kc * S + tb * 128: kc * S + (tb + 1) * 128], wv[:, kc * 512:(kc + 1) * 512],
                        kc == 0, kc == 1, [wvb, ckvn.bufs[kc * 4 + tb // 4]], [self.psb[b]])
            vv = vt.ap[:, tb * 520:(tb + 1) * 520].rearrange("p (a b) -> p a b", b=65)[:, :, 0:64]
            T.op("dve", lambda e, vv=vv, b=b: e.tensor_copy(out=vv, in_=self.PS(b).rearrange("p (a b) -> p a b", b=64)),
                 reads=[self.psb[b]], writes=[vt.bufs[tb]])

        for i in range(16 + 2):
            if i < 16:
                band_part(i)
                v_part(i)
            if i >= 2:
                lin_part(i - 2)
        pooled.free()
        utm.free()
        band.free()
        qhs = [A.alloc(S * 2, BF16, 4, "qh%d" % i) for i in range(2)]
        khs = [A.alloc(S * 2, BF16, 4, "kh%d" % i) for i in range(2)]
        ost = A.alloc(16 * 128 * 2, BF16, 16, "ost")
        rec = A.alloc(64, F32, 1, "rec")
        self.pt = A.alloc(4 * 1024, BF16, 4, "pt")
        self.npt = 4
        self.ptctr = 0
        scale = 96 ** -0.5

        def proj(h):
            qh, kh = qhs[h % 2], khs[h % 2]
            wq, wqb = self.wget(("wq", h))
            units = []
            for tc in range(4):
                def u1(tc=tc):
                    ba, bb = self.bank("side"), self.bank("side")
                    for ab, b_ in ((0, ba), (1, bb)):
                        for kc in range(2):
                            self.mm(self.PS(b_, 0, 512, 0, 96), wq[:, (ab * 2 + kc) * 96:(ab * 2 + kc + 1) * 96],
                                    cqn.ap[:, kc * S + tc * 512: kc * S + (tc + 1) * 512], kc == 0, kc == 1,
                                    [wqb, cqn.bufs[kc * 4 + tc]], [self.psb[b_]])
                    qa = qh.ap[0:64, tc * 512:(tc + 1) * 512]
                    T.op("dve", lambda e, qa=qa, ba=ba: e.tensor_copy(out=qa, in_=self.PS(ba, 0, 512, 0, 64)), reads=[self.psb[ba]], writes=[qh.bufs[tc]])
                    self.rope(ba, bb, cos, sins, tc, tmp, qh.ap[64:96, tc * 512:(tc + 1) * 512], [qh.bufs[tc]])

                def u2(tc=tc):
                    bk = self.bank("side")
                    for kc in range(2):
                        self.mm(self.PS(bk, 0, 512, 0, 64), wq[:, 384 + kc * 64: 384 + (kc + 1) * 64],
                                ckvn.ap[:, kc * S + tc * 512: kc * S + (tc + 1) * 512], kc == 0, kc == 1,
                                [wqb, ckvn.bufs[kc * 4 + tc]], [self.psb[bk]])
                    ka = kh.ap[0:64, tc * 512:(tc + 1) * 512]
                    T.op("dve", lambda e, ka=ka, bk=bk: e.tensor_copy(out=ka, in_=self.PS(bk, 0, 512, 0, 64)), reads=[self.psb[bk]], writes=[kh.bufs[tc]])
                    ka2 = kh.ap[64:96, tc * 512:(tc + 1) * 512]
                    kra = kr.ap[64:96, tc * 512:(tc + 1) * 512]
                    T.op("dve", lambda e, ka2=ka2, kra=kra: e.tensor_copy(out=ka2, in_=kra), reads=[kr.bufs[tc]], writes=[kh.bufs[tc]])
                units += [u1, u2]
            return units

        def attn(h, side):
            qh, kh = qhs[h % 2], khs[h % 2]

            def qk_fn(m, j, c, sb):
                self.mm(self.PS(sb), kh.ap[0:96, j * 128:(j + 1) * 128], qh.ap[0:96, c * 512:(c + 1) * 512], True, True,
                        [kh.bufs[j // 4], qh.bufs[c]], [self.psb[sb]])

            def vt_fn(m, j):
                return (vt.ap[:, j * 520 + h * 65: j * 520 + h * 65 + 65], [vt.bufs[j]])

            def done(c, m, accs):
                hp = h % 2
                ab = accs[0][0]
                acc3 = self.PS(ab, 0, 260).rearrange("p (a b) -> p a b", b=65)
                T.op("dve", lambda e, acc3=acc3: e.reciprocal(rec.ap[:, 0:4], acc3[:, :, 64]), reads=[self.psb[ab]], writes=rec.bufs)
                ov = ost.ap[:, c * 512:(c + 1) * 512].rearrange("p (a b) -> p a b", b=128)[:, :, hp * 64:(hp + 1) * 64]
                rb_ = rec.ap[:, 0:4].unsqueeze(2).to_broadcast([128, 4, 64])
                T.op("dve", lambda e, acc3=acc3, ov=ov, rb_=rb_: e.tensor_tensor(out=ov, in0=acc3[:, :, 0:64], in1=rb_, op=ALU.mult),
                     reads=[self.psb[ab]] + rec.bufs, writes=[ost.bufs[c * 4 + q_] for q_ in range(4)])
            self.attention_core(1, 96, 64, qk_fn, vt_fn, scale, lambda m, j, c, sb: None, done, side_units=side)
            if h % 2 == 1:
                self.transpose_out(ost, lambda tc: self.HNap(4 + h // 2, tc * 512, (tc + 1) * 512), lambda tc: [catb[4 + h // 2][tc]])

        self.bank_groups.update({"s": [0, 1, 2, 3], "acc": [4, 5], "side": [6, 7]})
        for u in proj(0):
            u()
        for h in range(8):
            side = proj(h + 1) if h + 1 < 8 else None
            attn(h, side)
        for r_ in [ost, rec, self.pt, vt, sq, tmp, cos, sins, kr, cqn, ckvn] + qhs + khs:
            r_.free()
        self.resid_proj("wo0", lambda kc, tc: self.HNap(kc, tc * 512, (tc + 1) * 512), lambda kc, tc: [catb[kc][tc]], NCH)

    def rope(self, ba, bb, cos, sins, tc, tmp, out_ap, out_bufs):
        T = self.T
        t1 = tmp.ap[64:96, 512:1024]
        t2 = tmp.ap[64:96, 1024:1536]
        ca = cos.ap[64:96, tc * 512:(tc + 1) * 512]
        sa = sins.ap[64:96, tc * 512:(tc + 1) * 512]
        T.op("dve", lambda e: e.tensor_tensor(out=t1, in0=self.PS(ba, 0, 512, 64, 96), in1=ca, op=ALU.mult),
             reads=[self.psb[ba]] + cos.bufs, writes=[tmp.bufs[1]])
        T.op("dve", lambda e: e.tensor_tensor(out=t2, in0=self.PS(bb, 0, 512, 64, 96), in1=sa, op=ALU.mult),
             reads=[self.psb[bb]] + sins.bufs, writes=[tmp.bufs[2]])
        T.op("dve", lambda e: e.tensor_tensor(out=out_ap, in0=t1, in1=t2, op=ALU.add),
             reads=[tmp.bufs[1], tmp.bufs[2]], writes=out_bufs)

    def transpose_out(self, ost, dst_fn, dst_bufs_fn):
        T = self.T
        for tc in range(4):
            b = self.bank()
            pbf = self.PS(b).bitcast(BF16)
            for t4 in range(4):
                tb = tc * 4 + t4
                T.op("pe", lambda e, tb=tb, t4=t4, pbf=pbf: e.transpose(pbf[:, t4 * 128:(t4 + 1) * 128], ost.ap[:, tb * 128:(tb + 1) * 128], self.ident),
                     reads=[ost.bufs[tb]], writes=[self.psb[b]], inc=(t4 == 3))
            o = dst_fn(tc)
            T.op("dve", lambda e, o=o, pbf=pbf: e.tensor_copy(out=o, in_=pbf[:, 0:512]), reads=[self.psb[b]], writes=dst_bufs_fn(tc))

    def layer1(self):
        T = self.T
        A = self.arena
        HNf = lambda kc, tc: self.HNap(kc, tc * 512, (tc + 1) * 512)
        self.rmsnorm("attn_g1", HNf, lambda kc, tc: [self.HNb[kc][tc]])
        otr = A.alloc(2 * S * 2, BF16, 8, "otr")
        qa = [[A.alloc(S * 2, BF16, 5, "qa%d%d" % (i, m)) for m in range(2)] for i in range(2)]
        kp = [[A.alloc(S * 2, BF16, 5, "kp%d%d" % (i, m)) for m in range(2)] for i in range(2)]
        kn = [[A.alloc(S * 2, BF16, 5, "kn%d%d" % (i, m)) for m in range(2)] for i in range(2)]
        cd = A.alloc(8 * 128 * 2, BF16, 1, "cdiag")
        vhs = [A.alloc(16 * 129 * 2, BF16, 16, "vh1%d" % i) for i in range(2)]
        ost = A.alloc(16 * 128 * 2, BF16, 16, "ost1")
        stg = A.alloc(1032 * 4, F32, 1, "stg")
        odd = [A.alloc(512 * 4, F32, 1, "od%d" % i) for i in range(2)]
        sm = A.alloc(256, F32, 1, "sm")
        sms = [A.alloc(64, F32, 1, "sms%d" % i) for i in range(2)]
        self.pt = A.alloc(6 * 1024, BF16, 6, "pt1")
        self.npt = 6
        self.ptctr = 0
        gsub = self.gsub
        qo, _ = self.clay["qext"]
        ko, _ = self.clay["kext"]
        co, _ = self.clay["cdiag"]
        kextb = Buf("kext")
        kext_ap = qa[0][0].ap[96:100, :]
        for i in range(2):
            for m in range(2):
                T.dma("pool", lambda e, i=i, m=m: e.dma_start(out=qa[i][m].ap[64:68, :], in_=self.cflat[64:68, qo:qo + S]), writes=[qa[i][m].bufs[4]])
        T.dma("pool", lambda e: e.dma_start(out=kext_ap, in_=self.cflat[64:68, ko:ko + S]), writes=[kextb])
        T.dma("pool", lambda e: e.dma_start(out=cd.ap, in_=self.cflat[:, co:co + 1024]), writes=cd.bufs)
        for i in range(2):
            T.op("dve", lambda e, i=i: e.memset(vhs[i].ap[:, 0:16 * 129].rearrange("p (a b) -> p a b", b=129)[:, :, 128], 1.0), writes=vhs[i].bufs)
        a3 = stg.ap[:, 0:1032].rearrange("p (a b) -> p a b", b=129)

        def proj(h):
            bs = h % 2
            w, wb = self.wget(("wqk", h))
            wv, wvb = self.wget(("wvv", h), hold=1)
            slope = 2.0 ** (-(h + 1))
            units = []

            def u0():
                for m in range(2):
                    for (kt, sg) in ((kp[bs][m], slope), (kn[bs][m], -slope)):
                        T.op("dve", lambda e, kt=kt, sg=sg: e.tensor_scalar(kt.ap[64:68, :], kext_ap, sg, None, ALU.mult),
                             reads=[kextb], writes=[kt.bufs[4]])
            units.append(u0)
            for tc in range(4):
                def uq(tc=tc):
                    b = self.bank("side")
                    for kc in range(NCH):
                        self.mm(self.PS(b), w[:, kc * 256: kc * 256 + 128], HNf(kc, tc), kc == 0, kc == NCH - 1,
                                [wb, self.HNb[kc][tc]], [self.psb[b]])
                    for m in range(2):
                        da = qa[bs][m].ap[0:64, tc * 512:(tc + 1) * 512]
                        T.op("dve", lambda e, da=da, b=b, m=m: e.tensor_scalar(da, self.PS(b, 0, 512, m * 64, (m + 1) * 64), 0.125, None, ALU.mult),
                             reads=[self.psb[b]], writes=[qa[bs][m].bufs[tc]])

                def uk(tc=tc):
                    b = self.bank("side")
                    for kc in range(NCH):
                        self.mm(self.PS(b), w[:, kc * 256 + 128: kc * 256 + 256], HNf(kc, tc), kc == 0, kc == NCH - 1,
                                [wb, self.HNb[kc][tc]], [self.psb[b]])
                    for m in range(2):
                        da = kp[bs][m].ap[0:64, tc * 512:(tc + 1) * 512]
                        T.op("dve", lambda e, da=da, b=b, m=m: e.tensor_copy(out=da, in_=self.PS(b, 0, 512, m * 64, (m + 1) * 64)),
                             reads=[self.psb[b]], writes=[kp[bs][m].bufs[tc]])
                    for m in range(2):
                        da = kp[bs][m].ap[0:64, tc * 512:(tc + 1) * 512]
                        dn = kn[bs][m].ap[0:64, tc * 512:(tc + 1) * 512]
                        T.op("dve", lambda e, da=da, dn=dn: e.tensor_copy(out=dn, in_=da), reads=[kp[bs][m].bufs[tc]], writes=[kn[bs][m].bufs[tc]])

                def uv(tc=tc):
                    b = self.bank("side")
                    for t4 in range(4):
                        tb = tc * 4 + t4
                        for kc in range(NCH):
                            self.mm(self.PS(b, t4 * 128, (t4 + 1) * 128), self.HNap(kc, tb * 128, (tb + 1) * 128), wv[:, kc * 128:(kc + 1) * 128],
                                    (t4 == 0 and kc == 0), kc == NCH - 1, [wvb, self.HNb[kc][tc]], [self.psb[b]], inc=(kc == NCH - 1), skip_group_check=True)
                    vv = vhs[bs].ap[:, tc * 4 * 129:(tc * 4 + 4) * 129].rearrange("p (a b) -> p a b", b=129)[:, :, 0:128]
                    T.op("dve", lambda e, vv=vv, b=b: e.tensor_copy(out=vv, in_=self.PS(b).rearrange("p (a b) -> p a b", b=128)),
                         reads=[self.psb[b]], writes=[vhs[bs].bufs[tc * 4 + q_] for q_ in range(4)])
                units += [uq, uk, uv]
            return units

        def attn(h, side):
            bs = h % 2
            slope = 2.0 ** (-(h + 1))
            vh = vhs[bs]

            def qk_fn(m, j, c, sb):
                r = j - 4 * c
                kcols = slice(j * 128, (j + 1) * 128)
                kb_ = j // 4
                qt = qa[bs][m]

                def piece(kt, lo, hi, start, last):
                    self.mm(self.PS(sb, lo, hi), kt.ap[0:68, kcols], qt.ap[0:68, c * 512 + lo: c * 512 + hi], start, True,
                            [kt.bufs[kb_], kt.bufs[4], qt.bufs[c], qt.bufs[4]], [self.psb[sb]], inc=last, skip_group_check=True)
                if r < 0:
                    piece(kp[bs][m], 0, 512, True, True)
                elif r > 3:
                    piece(kn[bs][m], 0, 512, True, True)
                else:
                    first = True
                    if r > 0:
                        piece(kn[bs][m], 0, 128 * r, True, False)
                        first = False
                    piece(kp[bs][m], 128 * r, 128 * (r + 1), first, False)
                    self.mm(self.PS(sb, 128 * r, 128 * (r + 1)), self.ident, cd.ap[:, h * 128:(h + 1) * 128], False, True,
                            cd.bufs, [self.psb[sb]], inc=(r == 3), skip_group_check=True)
                    if r < 3:
                        piece(kp[bs][m], 128 * (r + 1), 512, False, True)

            def vt_fn(m, j):
                return (vh.ap[:, j * 129:(j + 1) * 129], [vh.bufs[j]])

            def done(c, m, accs):
                sb_ = stg.bufs
                odr = odd[c % 2]
                odc = odr.ap[:, 0:512].rearrange("p (a b) -> p a b", b=128)
                odb = odr.bufs
                sm2 = sms[c % 2]
                od3 = odc
                for (q0, n_) in ((0, 3), (3, 1)):
                    ab = accs[q0][0]
                    i0_ = m * 4 + q0
                    T.op("dve", lambda e, ab=ab, i0_=i0_, n_=n_: e.tensor_copy(out=stg.ap[:, i0_ * 129:(i0_ + n_) * 129], in_=self.PS(ab, 0, n_ * 129)),
                         reads=[self.psb[ab]], writes=sb_)
                if m == 0:
                    return
                T.op("dve", lambda e: e.reciprocal(sm.ap[:, 0:8], a3[:, :, 128]), reads=sb_, writes=sm.bufs)
                T.op("dve", lambda e: e.tensor_scalar(sm.ap[:, 4:8], sm.ap[:, 4:8], self.neglam[:, 0:1], None, ALU.mult), reads=sm.bufs, writes=sm.bufs)

                def bc(lo, hi):
                    return sm.ap[:, lo:hi].unsqueeze(2).to_broadcast([128, hi - lo, 128])
                o0 = a3[:, 0:4, 0:128]
                o1 = a3[:, 4:8, 0:128]
                T.op("dve", lambda e: e.tensor_tensor(out=od3, in0=o0, in1=bc(0, 4), op=ALU.mult), reads=sb_ + sm.bufs, writes=odb)
                T.op("dve", lambda e: e.tensor_tensor(out=o1, in0=o1, in1=bc(4, 8), op=ALU.mult), reads=sb_ + sm.bufs, writes=sb_)
                T.op("dve", lambda e: e.tensor_tensor(out=od3, in0=od3, in1=o1, op=ALU.add), reads=sb_ + odb, writes=odb)
                T.op("dve", lambda e: e.tensor_tensor(out=o1, in0=od3, in1=od3, op=ALU.mult), reads=odb, writes=sb_)
                T.op("dve", lambda e: e.tensor_reduce(out=sm2.ap[:, 8:12], in_=o1, op=ALU.add, axis=mybir.AxisListType.X), reads=sb_, writes=sm2.bufs)

                def part_b():
                    T.op("act", lambda e: e.activation(out=sm2.ap[:, 8:12], in_=sm2.ap[:, 8:12], func=AF.Ln, scale=1.0 / 128, bias=self.epsb),
                         reads=sm2.bufs, writes=sm2.bufs)
                    T.op("act", lambda e: e.activation(out=sm2.ap[:, 8:12], in_=sm2.ap[:, 8:12], func=AF.Exp, scale=-0.5), reads=sm2.bufs, writes=sm2.bufs)
                    rsb = sm2.ap[:, 8:12].unsqueeze(2).to_broadcast([128, 4, 128])
                    T.op("dve", lambda e: e.tensor_tensor(out=odc, in0=odc, in1=rsb, op=ALU.mult), reads=odb + sm2.bufs, writes=odb)
                    ov = ost.ap[:, c * 512:(c + 1) * 512].rearrange("p (a b) -> p a b", b=128)
                    gb = gsub.unsqueeze(1).to_broadcast([128, 4, 128])
                    T.op("dve", lambda e, ov=ov: e.tensor_tensor(out=ov, in0=odc, in1=gb, op=ALU.mult), reads=odb, writes=[ost.bufs[c * 4 + q_] for q_ in range(4)])
                return part_b

            def keep_fn(c, j):
                if j < 4 * c:
                    dmin = 512 * c - (128 * j + 127)
                elif j > 4 * c + 3:
                    dmin = 128 * j - (512 * c + 511)
                else:
                    return True
                return slope * dmin < 104.0
            self.attention_core(2, 64, 128, qk_fn, vt_fn, 1.0, lambda m, j, c, sb: None, done, keep_fn=keep_fn, side_units=side)
            hg = h % 2
            self.transpose_out(ost, lambda tc: otr.ap[:, hg * S + tc * 512: hg * S + (tc + 1) * 512], lambda tc: [otr.bufs[hg * 4 + tc]])

        def outproj(g2):
            for n in range(NCH):
                wo, wob = self.wget(("wo1h", g2, n))
                q4 = self.bank_ctr.get("quad", 0)
                self.bank_ctr["quad"] = q4 + 1
                b0 = (q4 % 2) * 4
                for tc in range(4):
                    b = b0 + tc
                    for k2 in range(2):
                        self.mm(self.PS(b), wo[:, k2 * 128:(k2 + 1) * 128], otr.ap[:, k2 * S + tc * 512: k2 * S + (tc + 1) * 512], k2 == 0, k2 == 1,
                                [wob, otr.bufs[k2 * 4 + tc]], [self.psb[b]])
                hh = self.H.ap[:, n * S:(n + 1) * S]
                pw_ = self.ps[:, b0 * 512:(b0 + 4) * 512]
                T.op("dve", lambda e, hh=hh, pw_=pw_: e.tensor_tensor(out=hh, in0=hh, in1=pw_, op=ALU.add),
                     reads=[self.psb[b0 + i_] for i_ in range(4)], writes=[self.Hb[n][i_] for i_ in range(4)])

        self.bank_groups.update({"s": [0, 1, 2], "acc0": [3, 4], "acc1": [5, 6], "side": [7]})
        for u in proj(0):
            u()
        for h in range(8):
            side = proj(h + 1) if h + 1 < 8 else None
            attn(h, side)
            if h % 2 == 1:
                outproj(h // 2)
        for r_ in [otr, cd, ost, stg, sm, self.pt] + odd + sms + vhs + qa[0] + qa[1] + kp[0] + kp[1] + kn[0] + kn[1]:
            r_.free()

    def emit_all(self):
        T = self.T
        for s in range(self.nseq):
            self.bank_ctr = {}
            for tc in range(4):
                for c in range(NCH):
                    src = self.xT[s, c][:, tc * 512:(tc + 1) * 512]
                    dst = self.H.ap[:, c * S + tc * 512: c * S + (tc + 1) * 512]
                    T.dma("sp", lambda e, dst=dst, src=src: e.dma_start(out=dst, in_=src), writes=[self.Hb[c][tc]])
            stages = [("l0", self.layer0), ("f0", lambda: self.ffn(0)), ("l1", self.layer1), ("f1", lambda: self.ffn(1))]
            for name, fn in stages:
                fn()
                if self.stop_after == name:
                    break
            ob = self.arena.alloc(4 * 2048, F32, 4, "outstage")
            self.octr = 0
            if self.stop_after is None:
                self.final_norm(ob, s)
            else:
                for kc in range(NCH):
                    for tc in range(4):
                        src = self.Hap(kc, tc)
                        dst = self.outT[s, kc][:, tc * 512:(tc + 1) * 512]
                        t = T.dma("sp", lambda e, dst=dst, src=src: e.dma_start(out=dst, in_=src), reads=[self.Hb[kc][tc]])
                        self.out_tickets.append(t)
            ob.free()
        for t in self.out_tickets:
            T.wait_ticket("sp", t)

    def final_norm(self, ob, s):
        T = self.T
        A = self.arena
        sq = A.alloc(2 * 1024, BF16, 2, "sqf")
        rs = A.alloc(2 * 2048, F32, 2, "rstdf")
        g = self.cst("final_g")
        for tc in range(4):
            b = self.bank()
            for kc in range(NCH):
                sqa = sq.ap[:, (kc % 2) * 512:(kc % 2 + 1) * 512]
                src = self.Hap(kc, tc)
                T.op("act", lambda e, sqa=sqa, src=src: e.activation(out=sqa, in_=src, func=AF.Square),
                     reads=[self.Hb[kc][tc]], writes=[sq.bufs[kc % 2]])
                self.mm(self.PS(b), self.ones, sqa, kc == 0, kc == NCH - 1, [sq.bufs[kc % 2]], [self.psb[b]], inc=True)
            r = rs.ap[:, (tc % 2) * 512:(tc % 2 + 1) * 512]
            rb = rs.bufs[tc % 2]
            T.op("act", lambda e, r=r, b=b: e.activation(out=r, in_=self.PS(b), func=AF.Ln, scale=1.0 / D, bias=self.epsb),
                 reads=[self.psb[b]], writes=[rb])
            T.op("act", lambda e, r=r: e.activation(out=r, in_=r, func=AF.Exp, scale=-0.5), reads=[rb], writes=[rb])
            for kc in range(NCH):
                k = (tc * NCH + kc) % 4
                o = ob.ap[:, k * 512:(k + 1) * 512]
                src = self.Hap(kc, tc)
                T.op("dve", lambda e, o=o, src=src, kc=kc, r=r: e.scalar_tensor_tensor(
                    out=o, in0=src, scalar=g[:, kc:kc + 1], in1=r, op0=ALU.mult, op1=ALU.mult),
                    reads=[self.Hb[kc][tc], rb], writes=[ob.bufs[k]])
                dst = self.outT[s, kc][:, tc * 512:(tc + 1) * 512]
                t = T.dma("sp", lambda e, dst=dst, o=o: e.dma_start(out=dst, in_=o), reads=[ob.bufs[k]])
                self.out_tickets.append(t)
        sq.free()
        rs.free()

    def build(self):
        nc = bass.Bass("TRN2", target_bir_lowering=False)
        self.nc = nc
        self.xT = nc.dram_tensor("xT", [self.nseq, NCH, 128, S], F32, kind="ExternalInput").ap()
        self.wflat = nc.dram_tensor("wflat", [128, self.wtot], F32, kind="ExternalInput").ap()
        self.cflat = nc.dram_tensor("cflat", [128, self.ctot], F32, kind="ExternalInput").ap()
        self.outT = nc.dram_tensor("outT", [self.nseq, NCH, 128, S], F32, kind="ExternalOutput").ap()
        npers = self.clay["kb"][0] + 256
        with ExitStack() as st:
            arena_t = st.enter_context(nc.sbuf_tensor("arena", [128, ARENA_KB * 512], BF16))
            self.ps = st.enter_context(nc.psum_tensor("ps", [128, 8 * 512], F32))
            T = Tracker()
            self.T = T
            for name in ("pe", "act", "dve", "pool", "sp"):
                T.eng[name] = Eng(name, st.enter_context(nc.semaphore("sem_" + name)))
            for q in ("sp", "pool"):
                for i in range(12):
                    T.dsems[q].append(DSem(st.enter_context(nc.semaphore("dsem_%s%d" % (q, i)))))
            block = st.enter_context(nc.Block())

            for dry in (True, False):
                T.dry = dry
                self.arena = Arena(arena_t, ARENA_KB * 1024)
                A = self.arena
                self.psb = [Buf("ps%d" % i) for i in range(8)]
                self.bank_groups = {"g": list(range(8)), "s": [0, 1, 2, 3], "acc": [4, 5, 6]}
                self.bank_ctr = {}
                self.out_tickets = []
                self.wuse = 0
                self.wissued = 0
                cp = A.alloc(npers * 4, F32, 1, "cpers")
                self.cpers = cp.ap
                misc = A.alloc(128 * 2 * 2 + 64 + 128 * 4 + 64 * 4 * 2 + 64, BF16, 1, "misc")
                self.ones = misc.ap[:, 0:128]
                self.ident = misc.ap[:, 128:256]
                mf = misc.ap[:, 256:].bitcast(F32)
                self.epsb = mf[:, 0:1]
                self.neglam = mf[:, 1:2]
                lamt = mf[:, 2:8]
                self.gsub = mf[:, 16:144]
                ltmp = mf[:, 144:272]
                self.wring = A.alloc(NSLOT * SLOT * 2, BF16, NSLOT, "wring")
                self.H = A.alloc(NCH * S * 4, F32, 1, "H")
                self.Hb = [[Buf("H%d_%d" % (c, t)) for t in range(4)] for c in range(NCH)]
                self.HN = A.alloc(NCH * S * 2, BF16, 1, "HN")
                self.HNb = [[Buf("HN%d_%d" % (c, t)) for t in range(4)] for c in range(NCH)]
                T.dma("sp", lambda e: e.dma_start(out=self.cpers, in_=self.cflat[:, 0:npers]), writes=cp.bufs)
                T.op("dve", lambda e: e.memset(self.ones, 1.0), writes=misc.bufs)
                T.op("dve", lambda e: e.memset(self.epsb, EPS), writes=misc.bufs)
                T.op("dve", lambda e: e.tensor_copy(out=self.ident, in_=self.cst("ident")), reads=cp.bufs, writes=misc.bufs)
                lam_init = 0.8 - 0.6 * math.exp(-0.3 * 1)
                T.op("dve", lambda e: e.tensor_scalar(self.gsub, self.cst("subln_g"), 1.0 - lam_init, None, ALU.mult), reads=cp.bufs, writes=misc.bufs)
                for i, (a, b_) in enumerate((("diff_lambda_q1", "diff_lambda_k1"), ("diff_lambda_q2", "diff_lambda_k2"))):
                    T.op("dve", lambda e, a=a, b_=b_, i=i: e.tensor_tensor(out=ltmp[:, i * 64:(i + 1) * 64], in0=self.cst(a), in1=self.cst(b_), op=ALU.mult),
                         reads=cp.bufs, writes=misc.bufs)
                    T.op("dve", lambda e, i=i: e.reduce_sum(lamt[:, i:i + 1], ltmp[:, i * 64:(i + 1) * 64], axis=mybir.AxisListType.X),
                         reads=misc.bufs, writes=misc.bufs)
                T.op("act", lambda e: e.activation(out=lamt[:, 0:2], in_=lamt[:, 0:2], func=AF.Exp), reads=misc.bufs, writes=misc.bufs)
                T.op("dve", lambda e: e.scalar_tensor_tensor(out=self.neglam, in0=lamt[:, 1:2], scalar=-lam_init, in1=lamt[:, 0:1], op0=ALU.add, op1=ALU.subtract),
                     reads=misc.bufs, writes=misc.bufs)
                self.emit_all()

            progs = {n: T.eng[n].prog for n in T.eng}
            sems = {n: T.eng[n].sem for n in T.eng}

            def replay(name):
                def run(e):
                    sem = sems[name]
                    for item in progs[name]:
                        if item[0] == "wait":
                            e.wait_ge(item[1], item[2])
                        elif item[0] == "op":
                            ins = item[1](e)
                            if item[2]:
                                ins.then_inc(sem, 1)
                        else:
                            item[1](e).then_inc(item[2], 16)
                return run

            block.tensor(replay("pe"))
            block.scalar(replay("act"))
            block.vector(replay("dve"))
            block.gpsimd(replay("pool"))
            block.sync(replay("sp"))
        return nc


_CACHE = {}


def _get_program(nseq, stop_after=None):
    key = (nseq, stop_after)
    if key not in _CACHE:
        p = Program(nseq, stop_after)
        p.build()
        _CACHE[key] = p
    return _CACHE[key]


def kernel(**inputs):
    inp = {k: np.asarray(v) for k, v in inputs.items()}
    x = inp["x"]
    B = x.shape[0]
    ncores = int(os.environ.get("KNCORES", NCORES))
    stop_after = os.environ.get("KSTOP") or None
    nseq = B // ncores
    prog = _get_program(nseq, stop_after)
    _, _, wflat = weight_catalog(inp)
    _, _, cflat = const_catalog(inp)
    in_maps = []
    for c in range(ncores):
        xs = x[c * nseq:(c + 1) * nseq]
        xT = np.ascontiguousarray(xs.transpose(0, 2, 1)).reshape(nseq, NCH, 128, S)
        in_maps.append({"xT": xT, "wflat": wflat, "cflat": cflat})
    res = run_bass_kernel_spmd(prog.nc, in_maps, core_ids=list(range(ncores)))
    outs = []
    for c in range(ncores):
        oT = np.asarray(res.results[c]["outT"]).reshape(nseq, D, S)
        outs.append(oT.transpose(0, 2, 1))
    return np.ascontiguousarray(np.concatenate(outs, axis=0)).astype(np.float32, copy=False)
```

```python
import math
import os
import numpy as np
import concourse.bass as bass
import concourse.mybir as mybir
from concourse.bass_utils import run_bass_kernel_spmd
from contextlib import ExitStack

F32 = mybir.dt.float32
BF16 = mybir.dt.bfloat16
AF = mybir.ActivationFunctionType
ALU = mybir.AluOpType

S = 2048
D = 1024
NCH = 8
DFF = 2816
NF = 22
EPS = 1e-6
NCORES = 8
SLOT = 2048
NSLOT = 4
ARENA_KB = 207


def _tile_kc(w, cols):
    K = w.shape[0]
    sub = w[:, cols].reshape(K // 128, 128, len(cols))
    return np.ascontiguousarray(sub.transpose(1, 0, 2)).reshape(128, -1)


def weight_catalog(inp=None):
    items = []

    def add(key, F, fn):
        items.append((key, F, fn))

    ar = np.arange
    for o in range(4, 8):
        add(("win", o), 8 * 128, lambda o=o: _tile_kc(inp["hyb_w_in"][0], ar(o * 128, (o + 1) * 128)))
    for half in range(2):
        add(("wpool", half), 8 * 256, lambda half=half: _tile_kc(inp["hyb_w_in"][0], ar(half * 256, (half + 1) * 256)))
    swap = (ar(32) + 16) % 32
    add(("wkr", 0), 8 * 96, lambda: _tile_kc(inp["hyb_w_in"][0], np.concatenate([ar(64), 1024 + ar(32)])))
    add(("wkr", 1), 8 * 96, lambda: _tile_kc(inp["hyb_w_in"][0], np.concatenate([ar(64), 1024 + swap])))
    add(("poolw",), 4 * 128, lambda: np.ascontiguousarray(inp["pool_w"][0].transpose(1, 0, 2)).reshape(128, -1))
    add(("wv",), 2 * 512, lambda: _tile_kc(inp["mla_w_ukv"][0], (ar(512) // 64) * 128 + 64 + ar(512) % 64))
    for h in range(8):
        add(("wq", h), 2 * 2 * 96 + 2 * 64, lambda h=h: np.concatenate([
            _tile_kc(inp["mla_w_uq"][0], h * 96 + ar(96)),
            _tile_kc(inp["mla_w_uq"][0], h * 96 + np.concatenate([ar(64), 64 + swap])),
            _tile_kc(inp["mla_w_ukv"][0], h * 128 + ar(64))], axis=1))
    for n in range(8):
        add(("wo0", n), 8 * 128, lambda n=n: _tile_kc(inp["hyb_w_out"][0], ar(n * 128, (n + 1) * 128)))
    for l in range(2):
        for half in range(2):
            for f in range(NF):
                add(("wg", l, half, f), 8 * 256, lambda l=l, f=f: np.concatenate([
                    _tile_kc(inp["ffn_w_gate"][l], ar(f * 128, (f + 1) * 128)),
                    _tile_kc(inp["ffn_w_up"][l], ar(f * 128, (f + 1) * 128))], axis=1))
            for n in range(8):
                for fh in range(2):
                    add(("wd", l, half, n, fh), 11 * 128, lambda l=l, n=n, fh=fh: _tile_kc(
                        inp["ffn_w_down"][l][fh * 1408:(fh + 1) * 1408], ar(n * 128, (n + 1) * 128)))
        if l == 0:
            for h in range(8):
                add(("wqk", h), 8 * 256, lambda h=h: _tile_kc(
                    inp["diff_w_qkv"][0], np.concatenate([h * 128 + ar(128), 1024 + h * 128 + ar(128)])))
                add(("wvv", h), 8 * 128, lambda h=h: _tile_kc(inp["diff_w_qkv"][0], 2048 + h * 128 + ar(128)))
            for g2 in range(4):
                for n in range(8):
                    add(("wo1h", g2, n), 2 * 128, lambda n=n, g2=g2: _tile_kc(inp["diff_w_out"][0][g2 * 256:(g2 + 1) * 256], ar(n * 128, (n + 1) * 128)))
    cat = {}
    off = 0
    arrays = []
    seen = {}
    for key, F, fn in items:
        base = key
        if key[0] in ("wg", "wd"):
            base = (key[0], key[1]) + tuple(key[3:])
        if base in seen:
            cat[key] = seen[base]
            continue
        cat[key] = (off, F)
        seen[base] = (off, F)
        if inp is not None:
            a = fn()
            assert a.shape == (128, F), (key, a.shape, F)
            arrays.append(a.astype(np.float32, copy=False))
        off += F
    flat = np.concatenate(arrays, axis=1) if inp is not None else None
    return cat, off, flat


CONST_LAYOUT = {}


def const_catalog(inp=None):
    items = []

    def add(name, n, fn):
        items.append((name, n, fn))

    def fm(v):
        return np.ascontiguousarray(np.asarray(v).reshape(-1, 128).T)

    add("attn_g0", 8, lambda: fm(inp["attn_norm_g"][0]))
    add("attn_g1", 8, lambda: fm(inp["attn_norm_g"][1]))
    add("ffn_g0", 8, lambda: fm(inp["ffn_norm_g"][0]))
    add("ffn_g1", 8, lambda: fm(inp["ffn_norm_g"][1]))
    add("final_g", 8, lambda: fm(inp["final_norm_g"]))
    add("qn_g", 2, lambda: fm(inp["mla_q_norm_g"][0]))
    add("kvn_g", 2, lambda: fm(inp["mla_kv_norm_g"][0]))
    add("pool_scale", 4, lambda: fm(inp["pool_scale"][0]))
    add("subln_g", 128, lambda: np.broadcast_to(inp["diff_subln_g"][0][None, :], (128, 128)))
    for nm in ("diff_lambda_q1", "diff_lambda_k1", "diff_lambda_q2", "diff_lambda_k2"):
        add(nm, 64, lambda nm=nm: np.broadcast_to(inp[nm][0][None, :], (128, 64)))
    add("ident", 128, lambda: np.eye(128, dtype=np.float32))
    slopes = 2.0 ** (-np.arange(1, 9, dtype=np.float64))

    def kb():
        p = np.arange(128)[:, None, None, None]
        j = np.arange(16)[None, None, None, :]
        sg = np.array([1.0, -1.0])[None, None, :, None]
        m = slopes[None, :, None, None]
        return (sg * m * (128 * j + p - 1024.0)).reshape(128, -1).astype(np.float32)
    add("kb", 8 * 2 * 16, kb)
    pos = np.arange(S, dtype=np.float32)
    inv = (10000.0 ** (-np.arange(0, 32, 2, dtype=np.float32) / 32)).astype(np.float32)
    ang = pos[None, :] * inv[:, None]
    r = np.arange(128) % 32
    add("cos", S, lambda: np.cos(ang)[r % 16].astype(np.float32))
    add("sins", S, lambda: (np.sin(ang)[r % 16] * np.where(r < 16, -1.0, 1.0)[:, None]).astype(np.float32))
    q_ = np.arange(S)

    def qext():
        t = np.zeros((128, S), np.float32)
        t[64] = 128.0 * (q_ // 128)
        t[65] = q_ % 128
        t[66] = 1.0
        t[67] = 1.0
        return t

    def kext():
        t = np.zeros((128, S), np.float32)
        t[64] = -1.0
        t[65] = -1.0
        t[66] = 128.0 * (q_ // 128)
        t[67] = q_ % 128
        return t
    add("qext", S, qext)
    add("kext", S, kext)
    slopes8 = 2.0 ** (-np.arange(1, 9, dtype=np.float64))
    add("cdiag", 1024, lambda: (2.0 * slopes8[None, :, None] * np.minimum(np.arange(128)[None, None, :] - np.arange(128)[:, None, None], 0.0)
                                ).reshape(128, 1024).astype(np.float32))

    def band():
        out = np.zeros((128, 4, 5, 128), np.float32)
        for g, w in enumerate((2, 4, 8, 16)):
            for var, tb in enumerate((5, 0, 15)):
                t = np.arange(tb * 128, (tb + 1) * 128)
                lo = np.clip(t - w // 2, 0, S)
                hi = np.clip(t + w // 2, 0, S)
                cnt = (hi - lo).astype(np.float32)
                sidx = np.arange(tb * 128, (tb + 1) * 128)
                m = ((sidx[:, None] >= lo[None, :]) & (sidx[:, None] < hi[None, :])).astype(np.float32) / cnt[None, :]
                m -= np.eye(128, dtype=np.float32)
                out[:, g, var, :] = m
            tb = 5
            t = np.arange(tb * 128, (tb + 1) * 128)
            lo = t - w // 2
            hi = t + w // 2
            sprev = np.arange((tb - 1) * 128, tb * 128)
            snext = np.arange((tb + 1) * 128, (tb + 2) * 128)
            out[:, g, 3, :] = ((sprev[:, None] >= lo[None, :]) & (sprev[:, None] < hi[None, :])).astype(np.float32) / w
            out[:, g, 4, :] = ((snext[:, None] >= lo[None, :]) & (snext[:, None] < hi[None, :])).astype(np.float32) / w
        return out.reshape(128, -1)
    add("band", 4 * 5 * 128, band)

    lay = {}
    off = 0
    arrays = []
    for name, n, fn in items:
        lay[name] = (off, n)
        if inp is not None:
            a = np.asarray(fn(), dtype=np.float32)
            assert a.shape == (128, n), (name, a.shape)
            arrays.append(a)
        off += n
    flat = np.ascontiguousarray(np.concatenate(arrays, axis=1)) if inp is not None else None
    return lay, off, flat


class Buf:
    __slots__ = ("name", "w", "r")

    def __init__(self, name=""):
        self.name = name
        self.w = None
        self.r = {}


class Eng:
    def __init__(self, name, sem):
        self.name = name
        self.sem = sem
        self.count = 0
        self.waited = {}
        self.prog = []
        self.snaps = {}


class DSem:
    def __init__(self, h):
        self.h = h
        self.val = 0


class Tracker:
    def __init__(self):
        self.dry = False
        self.eng = {}
        self.dsems = {"sp": [], "pool": []}
        self.dnext = {"sp": 0, "pool": 0}
        self.semsnap = {}
        self.nops = 0

    def _deps(self, reads, writes):
        deps = {}

        def add(t):
            if t is None:
                return
            k = id(t[0])
            if k not in deps or deps[k][1] < t[1]:
                deps[k] = t
        for b in reads:
            add(b.w)
        for b in writes:
            add(b.w)
            for t in b.r.values():
                add(t)
        return deps

    def _emit_waits(self, E, deps):
        for k, (sem, val) in sorted(deps.items(), key=lambda kv: -kv[1][1]):
            if sem is E.sem:
                if E.name == "pe":
                    continue
                assert val <= E.count
            else:
                pe = self.eng["pe"]
                if sem is pe.sem and val > pe.count:
                    raise RuntimeError("wait on pending PE ticket from %s (deadlock risk)" % E.name)
            if E.waited.get(k, 0) >= val:
                continue
            E.prog.append(("wait", sem, val))
            E.waited[k] = val
            snap = self.semsnap.get(k, {}).get(val)
            if snap:
                for k2, v2 in snap.items():
                    if E.waited.get(k2, 0) < v2:
                        E.waited[k2] = v2

    def _mark(self, t, reads, writes):
        for b in writes:
            b.w = t
            b.r = {}
        k = id(t[0])
        for b in reads:
            if k not in b.r or b.r[k][1] < t[1]:
                b.r[k] = t

    def op(self, ename, fn, reads=(), writes=(), inc=True):
        if self.dry:
            return
        E = self.eng[ename]
        self._emit_waits(E, self._deps(reads, writes))
        E.prog.append(("op", fn, inc))
        self.nops += 1
        if inc:
            E.count += 1
            t = (E.sem, E.count)
            self.semsnap.setdefault(id(E.sem), {})[E.count] = dict(E.waited)
        else:
            assert ename == "pe"
            t = (E.sem, E.count + 1)
        self._mark(t, reads, writes)

    def dma(self, ename, fn, reads=(), writes=()):
        if self.dry:
            return None
        E = self.eng[ename]
        pool = self.dsems[ename]
        ds = pool[self.dnext[ename] % len(pool)]
        self.dnext[ename] += 1
        deps = self._deps(reads, writes)
        if ds.val > 0:
            deps[id(ds.h)] = (ds.h, ds.val)
        self._emit_waits(E, deps)
        E.prog.append(("dma", fn, ds.h))
        ds.val += 16
        t = (ds.h, ds.val)
        self.semsnap.setdefault(id(ds.h), {})[ds.val] = dict(E.waited)
        self._mark(t, reads, writes)
        return t

    def wait_ticket(self, ename, t):
        if self.dry or t is None:
            return
        E = self.eng[ename]
        self._emit_waits(E, {id(t[0]): t})


class Arena:
    def __init__(self, tensor, nbytes):
        self.t = tensor
        self.n = nbytes
        self.live = []
        self.dead = []

    def alloc(self, nbytes, dtype, nbufs=1, name=""):
        nbytes = (nbytes + 63) // 64 * 64
        cands = [0] + sorted(o + s for o, s, _ in self.live)
        off = None
        for c in cands:
            if c + nbytes > self.n:
                continue
            if all(c + nbytes <= o or c >= o + s for o, s, _ in self.live):
                off = c
                break
        if off is None:
            raise RuntimeError("arena full allocating %s (%d B); live=%s" % (name, nbytes, [(o, s) for o, s, _ in self.live]))
        bufs = [Buf("%s%d" % (name, i)) for i in range(nbufs)]
        newdead = []
        for o, s, bs in self.dead:
            if o < off + nbytes and off < o + s:
                for ob in bs:
                    for nb in bufs:
                        if ob.w is not None:
                            k = id(ob.w[0])
                            if k not in nb.r or nb.r[k][1] < ob.w[1]:
                                nb.r[k] = ob.w
                        for k, t in ob.r.items():
                            if k not in nb.r or nb.r[k][1] < t[1]:
                                nb.r[k] = t
                if not (o >= off and o + s <= off + nbytes):
                    newdead.append((o, s, bs))
            else:
                newdead.append((o, s, bs))
        self.dead = newdead
        self.live.append((off, nbytes, bufs))
        if not hasattr(self, "log"):
            self.log = {}
        self.log[name] = (off, nbytes)
        ap = self.t[:, off // 2:(off + nbytes) // 2]
        if dtype == F32:
            ap = ap.bitcast(F32)
        return Region(self, off, nbytes, ap, bufs)

    def free(self, reg):
        for i, (o, s, bs) in enumerate(self.live):
            if o == reg.off:
                self.dead.append(self.live.pop(i))
                return
        raise RuntimeError("double free")


class Region:
    def __init__(self, arena, off, nbytes, ap, bufs):
        self.arena = arena
        self.off = off
        self.nbytes = nbytes
        self.ap = ap
        self.bufs = bufs

    def free(self):
        self.arena.free(self)


class Program:
    def __init__(self, nseq, stop_after=None):
        self.nseq = nseq
        self.stop_after = stop_after
        self.wcat, self.wtot, _ = weight_catalog(None)
        self.clay, self.ctot, _ = const_catalog(None)
        self.wsched = []

    def bank(self, group="g"):
        lst = self.bank_groups[group]
        i = self.bank_ctr.get(group, 0)
        self.bank_ctr[group] = i + 1
        b = lst[i % len(lst)]
        return b

    def PS(self, b, lo=0, hi=512, p0=0, p1=128):
        return self.ps[p0:p1, b * 512 + lo:b * 512 + hi]

    def wget(self, key, hold=0):
        T = self.T
        if T.dry:
            self.wsched.append(key)
            off, F = self.wcat[key]
            return self.wring.ap[:, 0:F], self.wring.bufs[0]
        idx = self.wuse
        assert self.wsched[idx] == key, (self.wsched[idx], key)
        while self.wissued < min(len(self.wsched), idx + NSLOT - hold):
            k = self.wsched[self.wissued]
            slot = self.wissued % NSLOT
            off, F = self.wcat[k]
            assert F <= SLOT
            dst = self.wring.ap[:, slot * SLOT: slot * SLOT + F]
            src = self.wflat[:, off:off + F]
            T.dma("pool", lambda e, dst=dst, src=src: e.dma_start(out=dst, in_=src, max_dma_last_dim=8192),
                  writes=[self.wring.bufs[slot]])
            self.wissued += 1
        slot = idx % NSLOT
        self.wuse += 1
        off, F = self.wcat[key]
        return self.wring.ap[:, slot * SLOT: slot * SLOT + F], self.wring.bufs[slot]

    def cst(self, name):
        off, n = self.clay[name]
        return self.cpers[:, off:off + n]

    def mm(self, out, lhsT, rhs, start, stop, reads, writes, inc=None, **kw):
        if inc is None:
            inc = stop
        self.T.op("pe", lambda e: e.matmul(out, lhsT, rhs, start=start, stop=stop, **kw), reads=reads, writes=writes, inc=inc)

    def rmsnorm(self, gname, dst_fn, dst_bufs_fn, nfeat=D):
        T = self.T
        A = self.arena
        sq = A.alloc(2 * 1024, BF16, 2, "sq")
        rs = A.alloc(2 * 2048, F32, 2, "rstd")
        g = self.cst(gname)
        for tc in range(4):
            b = self.bank()
            for kc in range(NCH):
                sqa = sq.ap[:, (kc % 2) * 512:(kc % 2 + 1) * 512]
                src = self.Hap(kc, tc)
                T.op("act", lambda e, sqa=sqa, src=src: e.activation(out=sqa, in_=src, func=AF.Square),
                     reads=[self.Hb[kc][tc]], writes=[sq.bufs[kc % 2]])
                self.mm(self.PS(b), self.ones, sqa, kc == 0, kc == NCH - 1, [sq.bufs[kc % 2]], [self.psb[b]], inc=True)
            r = rs.ap[:, (tc % 2) * 512:(tc % 2 + 1) * 512]
            rb = rs.bufs[tc % 2]
            T.op("act", lambda e, r=r, b=b: e.activation(out=r, in_=self.PS(b), func=AF.Ln, scale=1.0 / nfeat, bias=self.epsb),
                 reads=[self.psb[b]], writes=[rb])
            T.op("act", lambda e, r=r: e.activation(out=r, in_=r, func=AF.Exp, scale=-0.5), reads=[rb], writes=[rb])
            for kc in range(NCH):
                o = dst_fn(kc, tc)
                src = self.Hap(kc, tc)
                T.op("dve", lambda e, o=o, src=src, kc=kc, r=r: e.scalar_tensor_tensor(
                    out=o, in0=src, scalar=g[:, kc:kc + 1], in1=r, op0=ALU.mult, op1=ALU.mult),
                    reads=[self.Hb[kc][tc], rb], writes=dst_bufs_fn(kc, tc))
        sq.free()
        rs.free()

    def Hap(self, kc, tc, lo=0, hi=512):
        return self.H.ap[:, kc * S + tc * 512 + lo: kc * S + tc * 512 + hi]

    def HNap(self, kc, t0, t1):
        return self.HN.ap[:, kc * S + t0: kc * S + t1]

    def resid_proj(self, wname, src_ap_fn, src_bufs_fn, nk):
        T = self.T
        for n in range(NCH):
            w, wb = self.wget((wname, n))
            for tc in range(4):
                b = self.bank()
                for kc in range(nk):
                    self.mm(self.PS(b), w[:, kc * 128:(kc + 1) * 128], src_ap_fn(kc, tc), kc == 0, kc == nk - 1,
                            [wb] + src_bufs_fn(kc, tc), [self.psb[b]])
                h = self.Hap(n, tc)
                T.op("dve", lambda e, h=h, b=b: e.tensor_tensor(out=h, in0=h, in1=self.PS(b), op=ALU.add),
                     reads=[self.psb[b]], writes=[self.Hb[n][tc]])

    def ffn(self, l):
        T = self.T
        A = self.arena
        self.rmsnorm("ffn_g%d" % l, lambda kc, tc: self.HNap(kc, tc * 512, (tc + 1) * 512), lambda kc, tc: [self.HNb[kc][tc]])
        gated = A.alloc(NF * 1024 * 2, BF16, NF * 2, "gated")
        sil = A.alloc(2 * 2048, F32, 2, "sil")
        for half in range(2):
            for f in range(NF):
                w, wb = self.wget(("wg", l, half, f))
                for t2 in range(2):
                    tc = half * 2 + t2
                    bg = self.bank()
                    bu = self.bank()
                    for which, b in ((0, bg), (1, bu)):
                        for kc in range(NCH):
                            self.mm(self.PS(b), w[:, which * 1024 + kc * 128: which * 1024 + (kc + 1) * 128],
                                    self.HNap(kc, tc * 512, (tc + 1) * 512), kc == 0, kc == NCH - 1,
                                    [wb, self.HNb[kc][tc]], [self.psb[b]])
                    k = (f * 2 + t2) % 2
                    sa = sil.ap[:, k * 512:(k + 1) * 512]
                    T.op("act", lambda e, sa=sa, bg=bg: e.activation(out=sa, in_=self.PS(bg), func=AF.Silu),
                         reads=[self.psb[bg]], writes=[sil.bufs[k]])
                    ga = gated.ap[:, f * 1024 + t2 * 512: f * 1024 + (t2 + 1) * 512]
                    T.op("dve", lambda e, ga=ga, sa=sa, bu=bu: e.tensor_tensor(out=ga, in0=sa, in1=self.PS(bu), op=ALU.mult),
                         reads=[sil.bufs[k], self.psb[bu]], writes=[gated.bufs[f * 2 + t2]])
            for n in range(NCH):
                w0, wb0 = self.wget(("wd", l, half, n, 0))
                w1, wb1 = self.wget(("wd", l, half, n, 1), hold=1)
                for t2 in range(2):
                    tc = half * 2 + t2
                    b = self.bank()
                    for f in range(NF):
                        w, wb = (w0, wb0) if f < 11 else (w1, wb1)
                        ff = f % 11
                        self.mm(self.PS(b), w[:, ff * 128:(ff + 1) * 128], gated.ap[:, f * 1024 + t2 * 512: f * 1024 + (t2 + 1) * 512],
                                f == 0, f == NF - 1, [wb, gated.bufs[f * 2 + t2]], [self.psb[b]])
                    h = self.Hap(n, tc)
                    T.op("dve", lambda e, h=h, b=b: e.tensor_tensor(out=h, in0=h, in1=self.PS(b), op=ALU.add),
                         reads=[self.psb[b]], writes=[self.Hb[n][tc]])
        gated.free()
        sil.free()

    def load_const(self, name, dtype=F32):
        off, n = self.clay[name]
        reg = self.arena.alloc(n * (4 if dtype == F32 else 2), dtype, 1, name)
        src = self.cflat[:, off:off + n]
        if dtype == F32:
            self.T.dma("sp", lambda e: e.dma_start(out=reg.ap, in_=src), writes=reg.bufs)
        else:
            self.T.dma("pool", lambda e: e.dma_start(out=reg.ap, in_=src, max_dma_last_dim=8192), writes=reg.bufs)
        return reg

    def attention_core(self, nmaps, kdim, vdim, qk_fn, vt_fn, scale, bias_fn, acc_done_fn, keep_fn=None, depth=3, side_units=None):
        T = self.T
        W1 = vdim + 1
        per_bank = 512 // W1
        nacc = nmaps * 4
        nbanks = (nacc + per_bank - 1) // per_bank
        per_pass = 4
        nbanks = (per_pass + per_bank - 1) // per_bank
        tiles = []
        for c in range(4):
            js = [j for j in range(16) if keep_fn is None or keep_fn(c, j)]
            for m in range(nmaps):
                for j in js:
                    tiles.append((c, j, m, j == js[0], j == js[-1]))
        inflight = {}
        chunk = {}
        deferred = []
        defer = 18

        def stage1(t):
            c, j, m, first, lastj = tiles[t]
            sb = self.bank("s")
            qk_fn(m, j, c, sb)
            bias = bias_fn(m, j, c, sb)
            pslot = self.ptctr % self.npt
            self.ptctr += 1
            pa = self.pt.ap[:, pslot * 512:(pslot + 1) * 512]
            pb = self.pt.bufs[pslot]
            if bias is None:
                T.op("act", lambda e, pa=pa, sb=sb: e.activation(out=pa, in_=self.PS(sb), func=AF.Exp, scale=scale),
                     reads=[self.psb[sb]], writes=[pb])
            else:
                bap, brd = bias
                T.op("act", lambda e, pa=pa, sb=sb, bap=bap: e.activation(out=pa, in_=self.PS(sb), func=AF.Exp, scale=scale, bias=bap),
                     reads=[self.psb[sb]] + brd, writes=[pb])
            inflight[t] = (pa, pb)

        def stage2(t):
            c, j, m, first, lastj = tiles[t]
            pa, pb = inflight.pop(t)
            if first:
                grp = "acc%d" % m if nmaps > 1 else "acc"
                abanks = [self.bank(grp) for _ in range(nbanks)]
                accs = [(abanks[qb // per_bank], (qb % per_bank) * W1) for qb in range(4)]
                chunk[(c, m)] = (accs, set())
            accs, started = chunk[(c, m)]
            va, vrd = vt_fn(m, j)
            for qb in range(4):
                ab, lo = accs[qb]
                st = ab not in started
                started.add(ab)
                self.mm(self.PS(ab, lo, lo + W1), pa[:, qb * 128:(qb + 1) * 128], va, st, lastj,
                        [pb] + vrd, [self.psb[ab]], inc=(qb == 3), skip_group_check=True)
            if lastj:
                later = acc_done_fn(c, m, accs)
                if later is not None:
                    deferred.append((t + depth + defer, later))
                del chunk[(c, m)]

        n = len(tiles)
        units = list(side_units or [])
        stride = max(1, n // (len(units) + 1)) if units else 1
        for t in range(n + depth):
            if t < n:
                stage1(t)
            if t >= depth:
                stage2(t - depth)
            if units and t >= 2 and (t - 2) % stride == 0:
                units.pop(0)()
            while deferred and deferred[0][0] <= t:
                deferred.pop(0)[1]()
        while units:
            units.pop(0)()
        while deferred:
            deferred.pop(0)[1]()

    def layer0(self):
        T = self.T
        A = self.arena
        HNf = lambda kc, tc: self.HNap(kc, tc * 512, (tc + 1) * 512)
        self.rmsnorm("attn_g0", HNf, lambda kc, tc: [self.HNb[kc][tc]])
        cqn = A.alloc(2 * S * 2, BF16, 8, "cqn")
        ckvn = A.alloc(2 * S * 2, BF16, 8, "ckvn")
        kr = A.alloc(S * 2, BF16, 4, "kr")
        cos = self.load_const("cos")
        sins = self.load_const("sins")
        band = self.load_const("band", BF16)
        utm = A.alloc(16 * 512 * 2, BF16, 16, "utm")
        tmp = A.alloc(3 * 2048, F32, 3, "tmp")
        sq = A.alloc(2 * 1024, BF16, 2, "sq0")

        for half in range(2):
            w, wb = self.wget(("wpool", half))
            for tb in range(16):
                b = self.bank()
                for kc in range(NCH):
                    self.mm(self.PS(b, 0, 256), self.HNap(kc, tb * 128, (tb + 1) * 128), w[:, kc * 256:(kc + 1) * 256], kc == 0, kc == NCH - 1,
                            [wb, self.HNb[kc][tb // 4]], [self.psb[b]])
                ua = utm.ap[:, tb * 512 + half * 256: tb * 512 + (half + 1) * 256]
                T.op("dve", lambda e, ua=ua, b=b: e.tensor_copy(out=ua, in_=self.PS(b, 0, 256)), reads=[self.psb[b]], writes=[utm.bufs[tb]])
        for (o0, gname, dst) in ((4, "qn_g", cqn), (6, "kvn_g", ckvn)):
            w0, wb0 = self.wget(("win", o0))
            w1, wb1 = self.wget(("win", o0 + 1), hold=1)
            g = self.cst(gname)
            for tc in range(4):
                bs = [self.bank(), self.bank()]
                for ci, (w_, wb_) in enumerate(((w0, wb0), (w1, wb1))):
                    for kc in range(NCH):
                        self.mm(self.PS(bs[ci]), w_[:, kc * 128:(kc + 1) * 128], HNf(kc, tc), kc == 0, kc == NCH - 1,
                                [wb_, self.HNb[kc][tc]], [self.psb[bs[ci]]])
                bm = self.bank()
                for ci in range(2):
                    sqa = sq.ap[:, ci * 512:(ci + 1) * 512]
                    T.op("act", lambda e, sqa=sqa, b=bs[ci]: e.activation(out=sqa, in_=self.PS(b), func=AF.Square),
                         reads=[self.psb[bs[ci]]], writes=[sq.bufs[ci]])
                    self.mm(self.PS(bm), self.ones, sqa, ci == 0, ci == 1, [sq.bufs[ci]], [self.psb[bm]], inc=True)
                r = tmp.ap[:, 0:512]
                T.op("act", lambda e, r=r, bm=bm: e.activation(out=r, in_=self.PS(bm), func=AF.Ln, scale=1.0 / 256, bias=self.epsb),
                     reads=[self.psb[bm]], writes=[tmp.bufs[0]])
                T.op("act", lambda e, r=r: e.activation(out=r, in_=r, func=AF.Exp, scale=-0.5), reads=[tmp.bufs[0]], writes=[tmp.bufs[0]])
                for ci in range(2):
                    o = dst.ap[:, ci * S + tc * 512: ci * S + (tc + 1) * 512]
                    T.op("dve", lambda e, o=o, b=bs[ci], ci=ci, r=r, g=g: e.scalar_tensor_tensor(
                        out=o, in0=self.PS(b), scalar=g[:, ci:ci + 1], in1=r, op0=ALU.mult, op1=ALU.mult),
                        reads=[self.psb[bs[ci]], tmp.bufs[0]], writes=[dst.bufs[ci * 4 + tc]])
        wa, wab = self.wget(("wkr", 0))
        wbb, wbbb = self.wget(("wkr", 1), hold=1)
        if True:
            for tc in range(4):
                ba, bb = self.bank(), self.bank()
                for (w_, wb_, b_) in ((wa, wab, ba), (wbb, wbbb, bb)):
                    for kc in range(NCH):
                        self.mm(self.PS(b_, 0, 512, 0, 96), w_[:, kc * 96:(kc + 1) * 96], HNf(kc, tc), kc == 0, kc == NCH - 1,
                                [wb_, self.HNb[kc][tc]], [self.psb[b_]])
                self.rope(ba, bb, cos, sins, tc, tmp, kr.ap[64:96, tc * 512:(tc + 1) * 512], [kr.bufs[tc]])
        cat = self.HN
        catb = self.HNb
        pw, pwb = self.wget(("poolw",))
        wv, wvb = self.wget(("wv",), hold=1)
        pooled = A.alloc(4 * 512 * 2, BF16, 4, "pooled")
        vt = A.alloc(16 * 8 * 65 * 2, BF16, 16, "vt")
        T.op("dve", lambda e: e.memset(vt.ap[:, 0:16 * 520].rearrange("p (a b) -> p a b", b=65)[:, :, 64], 1.0), writes=vt.bufs)
        psc = self.cst("pool_scale")
        items = [(g, tc) for g in range(4) for tc in range(4)]

        def band_part(i):
            g, tc = items[i]
            b = self.bank()
            for t4 in range(4):
                tb = tc * 4 + t4
                srcs = []
                if tb > 0:
                    srcs.append((tb - 1, 3))
                srcs.append((tb, 1 if tb == 0 else (2 if tb == 15 else 0)))
                if tb < 15:
                    srcs.append((tb + 1, 4))
                for k_, (sbk, var) in enumerate(srcs):
                    self.mm(self.PS(b, t4 * 128, (t4 + 1) * 128), utm.ap[:, sbk * 512 + g * 128: sbk * 512 + (g + 1) * 128],
                            band.ap[:, (g * 5 + var) * 128:(g * 5 + var + 1) * 128], (t4 == 0 and k_ == 0), k_ == len(srcs) - 1,
                            [utm.bufs[sbk]] + band.bufs, [self.psb[b]], inc=(k_ == len(srcs) - 1), skip_group_check=True)
            pa = pooled.ap[:, (i % 4) * 512:(i % 4 + 1) * 512]
            T.op("act", lambda e, pa=pa, b=b: e.activation(out=pa, in_=self.PS(b), func=AF.Copy), reads=[self.psb[b]], writes=[pooled.bufs[i % 4]])

        def lin_part(i):
            g, tc = items[i]
            pa = pooled.ap[:, (i % 4) * 512:(i % 4 + 1) * 512]
            b2 = self.bank()
            self.mm(self.PS(b2), pw[:, g * 128:(g + 1) * 128], pa, True, True, [pwb, pooled.bufs[i % 4]], [self.psb[b2]])
            o = self.HNap(g, tc * 512, (tc + 1) * 512)
            T.op("act", lambda e, o=o, b2=b2, g=g: e.activation(out=o, in_=self.PS(b2), func=AF.Identity, scale=psc[:, g:g + 1]),
                 reads=[self.psb[b2]], writes=[catb[g][tc]])

        def v_part(tb):
            b = self.bank()
            for kc in range(2):
                self.mm(self.PS(b), ckvn.ap[:, kc * S + tb * 128: kc * S + (tb + 1) * 128], wv[:, kc * 512:(kc + 1) * 512],
                        kc == 0, kc == 1, [wvb, ckvn.bufs[kc * 4 + tb // 4]], [self.psb[b]])
            vv = vt.ap[:, tb * 520:(tb + 1) * 520].rearrange("p (a b) -> p a b", b=65)[:, :, 0:64]
            T.op("dve", lambda e, vv=vv, b=b: e.tensor_copy(out=vv, in_=self.PS(b).rearrange("p (a b) -> p a b", b=64)),
                 reads=[self.psb[b]], writes=[vt.bufs[tb]])

        for i in range(16 + 2):
            if i < 16:
                band_part(i)
                v_part(i)
            if i >= 2:
                lin_part(i - 2)
        pooled.free()
        utm.free()
        band.free()
        qhs = [A.alloc(S * 2, BF16, 4, "qh%d" % i) for i in range(2)]
        khs = [A.alloc(S * 2, BF16, 4, "kh%d" % i) for i in range(2)]
        ost = A.alloc(16 * 128 * 2, BF16, 16, "ost")
        rec = A.alloc(64, F32, 1, "rec")
        self.pt = A.alloc(4 * 1024, BF16, 4, "pt")
        self.npt = 4
        self.ptctr = 0
        scale = 96 ** -0.5

        def proj(h):
            qh, kh = qhs[h % 2], khs[h % 2]
            wq, wqb = self.wget(("wq", h))
            units = []
            for tc in range(4):
                def u1(tc=tc):
                    ba, bb = self.bank("side"), self.bank("side")
                    for ab, b_ in ((0, ba), (1, bb)):
                        for kc in range(2):
                            self.mm(self.PS(b_, 0, 512, 0, 96), wq[:, (ab * 2 + kc) * 96:(ab * 2 + kc + 1) * 96],
                                    cqn.ap[:, kc * S + tc * 512: kc * S + (tc + 1) * 512], kc == 0, kc == 1,
                                    [wqb, cqn.bufs[kc * 4 + tc]], [self.psb[b_]])
                    qa = qh.ap[0:64, tc * 512:(tc + 1) * 512]
                    T.op("dve", lambda e, qa=qa, ba=ba: e.tensor_copy(out=qa, in_=self.PS(ba, 0, 512, 0, 64)), reads=[self.psb[ba]], writes=[qh.bufs[tc]])
                    self.rope(ba, bb, cos, sins, tc, tmp, qh.ap[64:96, tc * 512:(tc + 1) * 512], [qh.bufs[tc]])

                def u2(tc=tc):
                    bk = self.bank("side")
                    for kc in range(2):
                        self.mm(self.PS(bk, 0, 512, 0, 64), wq[:, 384 + kc * 64: 384 + (kc + 1) * 64],
                                ckvn.ap[:, kc * S + tc * 512: kc * S + (tc + 1) * 512], kc == 0, kc == 1,
                                [wqb, ckvn.bufs[kc * 4 + tc]], [self.psb[bk]])
                    ka = kh.ap[0:64, tc * 512:(tc + 1) * 512]
                    T.op("dve", lambda e, ka=ka, bk=bk: e.tensor_copy(out=ka, in_=self.PS(bk, 0, 512, 0, 64)), reads=[self.psb[bk]], writes=[kh.bufs[tc]])
                    ka2 = kh.ap[64:96, tc * 512:(tc + 1) * 512]
                    kra = kr.ap[64:96, tc * 512:(tc + 1) * 512]
                    T.op("dve", lambda e, ka2=ka2, kra=kra: e.tensor_copy(out=ka2, in_=kra), reads=[kr.bufs[tc]], writes=[kh.bufs[tc]])
                units += [u1, u2]
            return units

        def attn(h, side):
            qh, kh = qhs[h % 2], khs[h % 2]

            def qk_fn(m, j, c, sb):
                self.mm(self.PS(sb), kh.ap[0:96, j * 128:(j + 1) * 128], qh.ap[0:96, c * 512:(c + 1) * 512], True, True,
                        [kh.bufs[j // 4], qh.bufs[c]], [self.psb[sb]])

            def vt_fn(m, j):
                return (vt.ap[:, j * 520 + h * 65: j * 520 + h * 65 + 65], [vt.bufs[j]])

            def done(c, m, accs):
                hp = h % 2
                ab = accs[0][0]
                acc3 = self.PS(ab, 0, 260).rearrange("p (a b) -> p a b", b=65)
                T.op("dve", lambda e, acc3=acc3: e.reciprocal(rec.ap[:, 0:4], acc3[:, :, 64]), reads=[self.psb[ab]], writes=rec.bufs)
                ov = ost.ap[:, c * 512:(c + 1) * 512].rearrange("p (a b) -> p a b", b=128)[:, :, hp * 64:(hp + 1) * 64]
                rb_ = rec.ap[:, 0:4].unsqueeze(2).to_broadcast([128, 4, 64])
                T.op("dve", lambda e, acc3=acc3, ov=ov, rb_=rb_: e.tensor_tensor(out=ov, in0=acc3[:, :, 0:64], in1=rb_, op=ALU.mult),
                     reads=[self.psb[ab]] + rec.bufs, writes=[ost.bufs[c * 4 + q_] for q_ in range(4)])
            self.attention_core(1, 96, 64, qk_fn, vt_fn, scale, lambda m, j, c, sb: None, done, side_units=side)
            if h % 2 == 1:
                self.transpose_out(ost, lambda tc: self.HNap(4 + h // 2, tc * 512, (tc + 1) * 512), lambda tc: [catb[4 + h // 2][tc]])

        self.bank_groups.update({"s": [0, 1, 2, 3], "acc": [4, 5], "side": [6, 7]})
        for u in proj(0):
            u()
        for h in range(8):
            side = proj(h + 1) if h + 1 < 8 else None
            attn(h, side)
        for r_ in [ost, rec, self.pt, vt, sq, tmp, cos, sins, kr, cqn, ckvn] + qhs + khs:
            r_.free()
        self.resid_proj("wo0", lambda kc, tc: self.HNap(kc, tc * 512, (tc + 1) * 512), lambda kc, tc: [catb[kc][tc]], NCH)

    def rope(self, ba, bb, cos, sins, tc, tmp, out_ap, out_bufs):
        T = self.T
        t1 = tmp.ap[64:96, 512:1024]
        t2 = tmp.ap[64:96, 1024:1536]
        ca = cos.ap[64:96, tc * 512:(tc + 1) * 512]
        sa = sins.ap[64:96, tc * 512:(tc + 1) * 512]
        T.op("dve", lambda e: e.tensor_tensor(out=t1, in0=self.PS(ba, 0, 512, 64, 96), in1=ca, op=ALU.mult),
             reads=[self.psb[ba]] + cos.bufs, writes=[tmp.bufs[1]])
        T.op("dve", lambda e: e.tensor_tensor(out=t2, in0=self.PS(bb, 0, 512, 64, 96), in1=sa, op=ALU.mult),
             reads=[self.psb[bb]] + sins.bufs, writes=[tmp.bufs[2]])
        T.op("dve", lambda e: e.tensor_tensor(out=out_ap, in0=t1, in1=t2, op=ALU.add),
             reads=[tmp.bufs[1], tmp.bufs[2]], writes=out_bufs)

    def transpose_out(self, ost, dst_fn, dst_bufs_fn):
        T = self.T
        for tc in range(4):
            b = self.bank()
            pbf = self.PS(b).bitcast(BF16)
            for t4 in range(4):
                tb = tc * 4 + t4
                T.op("pe", lambda e, tb=tb, t4=t4, pbf=pbf: e.transpose(pbf[:, t4 * 128:(t4 + 1) * 128], ost.ap[:, tb * 128:(tb + 1) * 128], self.ident),
                     reads=[ost.bufs[tb]], writes=[self.psb[b]], inc=(t4 == 3))
            o = dst_fn(tc)
            T.op("dve", lambda e, o=o, pbf=pbf: e.tensor_copy(out=o, in_=pbf[:, 0:512]), reads=[self.psb[b]], writes=dst_bufs_fn(tc))

    def layer1(self):
        T = self.T
        A = self.arena
        HNf = lambda kc, tc: self.HNap(kc, tc * 512, (tc + 1) * 512)
        self.rmsnorm("attn_g1", HNf, lambda kc, tc: [self.HNb[kc][tc]])
        otr = A.alloc(2 * S * 2, BF16, 8, "otr")
        qa = [[A.alloc(S * 2, BF16, 5, "qa%d%d" % (i, m)) for m in range(2)] for i in range(2)]
        kp = [[A.alloc(S * 2, BF16, 5, "kp%d%d" % (i, m)) for m in range(2)] for i in range(2)]
        kn = [[A.alloc(S * 2, BF16, 5, "kn%d%d" % (i, m)) for m in range(2)] for i in range(2)]
        cd = A.alloc(8 * 128 * 2, BF16, 1, "cdiag")
        vhs = [A.alloc(16 * 129 * 2, BF16, 16, "vh1%d" % i) for i in range(2)]
        ost = A.alloc(16 * 128 * 2, BF16, 16, "ost1")
        stg = A.alloc(1032 * 4, F32, 1, "stg")
        odd = [A.alloc(512 * 4, F32, 1, "od%d" % i) for i in range(2)]
        sm = A.alloc(256, F32, 1, "sm")
        sms = [A.alloc(64, F32, 1, "sms%d" % i) for i in range(2)]
        self.pt = A.alloc(6 * 1024, BF16, 6, "pt1")
        self.npt = 6
        self.ptctr = 0
        gsub = self.gsub
        qo, _ = self.clay["qext"]
        ko, _ = self.clay["kext"]
        co, _ = self.clay["cdiag"]
        kextb = Buf("kext")
        kext_ap = qa[0][0].ap[96:100, :]
        for i in range(2):
            for m in range(2):
                T.dma("pool", lambda e, i=i, m=m: e.dma_start(out=qa[i][m].ap[64:68, :], in_=self.cflat[64:68, qo:qo + S]), writes=[qa[i][m].bufs[4]])
        T.dma("pool", lambda e: e.dma_start(out=kext_ap, in_=self.cflat[64:68, ko:ko + S]), writes=[kextb])
        T.dma("pool", lambda e: e.dma_start(out=cd.ap, in_=self.cflat[:, co:co + 1024]), writes=cd.bufs)
        for i in range(2):
            T.op("dve", lambda e, i=i: e.memset(vhs[i].ap[:, 0:16 * 129].rearrange("p (a b) -> p a b", b=129)[:, :, 128], 1.0), writes=vhs[i].bufs)
        a3 = stg.ap[:, 0:1032].rearrange("p (a b) -> p a b", b=129)

        def proj(h):
            bs = h % 2
            w, wb = self.wget(("wqk", h))
            wv, wvb = self.wget(("wvv", h), hold=1)
            slope = 2.0 ** (-(h + 1))
            units = []

            def u0():
                for m in range(2):
                    for (kt, sg) in ((kp[bs][m], slope), (kn[bs][m], -slope)):
                        T.op("dve", lambda e, kt=kt, sg=sg: e.tensor_scalar(kt.ap[64:68, :], kext_ap, sg, None, ALU.mult),
                             reads=[kextb], writes=[kt.bufs[4]])
            units.append(u0)
            for tc in range(4):
                def uq(tc=tc):
                    b = self.bank("side")
                    for kc in range(NCH):
                        self.mm(self.PS(b), w[:, kc * 256: kc * 256 + 128], HNf(kc, tc), kc == 0, kc == NCH - 1,
                                [wb, self.HNb[kc][tc]], [self.psb[b]])
                    for m in range(2):
                        da = qa[bs][m].ap[0:64, tc * 512:(tc + 1) * 512]
                        T.op("dve", lambda e, da=da, b=b, m=m: e.tensor_scalar(da, self.PS(b, 0, 512, m * 64, (m + 1) * 64), 0.125, None, ALU.mult),
                             reads=[self.psb[b]], writes=[qa[bs][m].bufs[tc]])

                def uk(tc=tc):
                    b = self.bank("side")
                    for kc in range(NCH):
                        self.mm(self.PS(b), w[:, kc * 256 + 128: kc * 256 + 256], HNf(kc, tc), kc == 0, kc == NCH - 1,
                                [wb, self.HNb[kc][tc]], [self.psb[b]])
                    for m in range(2):
                        da = kp[bs][m].ap[0:64, tc * 512:(tc + 1) * 512]
                        T.op("dve", lambda e, da=da, b=b, m=m: e.tensor_copy(out=da, in_=self.PS(b, 0, 512, m * 64, (m + 1) * 64)),
                             reads=[self.psb[b]], writes=[kp[bs][m].bufs[tc]])
                    for m in range(2):
                        da = kp[bs][m].ap[0:64, tc * 512:(tc + 1) * 512]
                        dn = kn[bs][m].ap[0:64, tc * 512:(tc + 1) * 512]
                        T.op("dve", lambda e, da=da, dn=dn: e.tensor_copy(out=dn, in_=da), reads=[kp[bs][m].bufs[tc]], writes=[kn[bs][m].bufs[tc]])

                def uv(tc=tc):
                    b = self.bank("side")
                    for t4 in range(4):
                        tb = tc * 4 + t4
                        for kc in range(NCH):
                            self.mm(self.PS(b, t4 * 128, (t4 + 1) * 128), self.HNap(kc, tb * 128, (tb + 1) * 128), wv[:, kc * 128:(kc + 1) * 128],
                                    (t4 == 0 and kc == 0), kc == NCH - 1, [wvb, self.HNb[kc][tc]], [self.psb[b]], inc=(kc == NCH - 1), skip_group_check=True)
                    vv = vhs[bs].ap[:, tc * 4 * 129:(tc * 4 + 4) * 129].rearrange("p (a b) -> p a b", b=129)[:, :, 0:128]
                    T.op("dve", lambda e, vv=vv, b=b: e.tensor_copy(out=vv, in_=self.PS(b).rearrange("p (a b) -> p a b", b=128)),
                         reads=[self.psb[b]], writes=[vhs[bs].bufs[tc * 4 + q_] for q_ in range(4)])
                units += [uq, uk, uv]
            return units

        def attn(h, side):
            bs = h % 2
            slope = 2.0 ** (-(h + 1))
            vh = vhs[bs]

            def qk_fn(m, j, c, sb):
                r = j - 4 * c
                kcols = slice(j * 128, (j + 1) * 128)
                kb_ = j // 4
                qt = qa[bs][m]

                def piece(kt, lo, hi, start, last):
                    self.mm(self.PS(sb, lo, hi), kt.ap[0:68, kcols], qt.ap[0:68, c * 512 + lo: c * 512 + hi], start, True,
                            [kt.bufs[kb_], kt.bufs[4], qt.bufs[c], qt.bufs[4]], [self.psb[sb]], inc=last, skip_group_check=True)
                if r < 0:
                    piece(kp[bs][m], 0, 512, True, True)
                elif r > 3:
                    piece(kn[bs][m], 0, 512, True, True)
                else:
                    first = True
                    if r > 0:
                        piece(kn[bs][m], 0, 128 * r, True, False)
                        first = False
                    piece(kp[bs][m], 128 * r, 128 * (r + 1), first, False)
                    self.mm(self.PS(sb, 128 * r, 128 * (r + 1)), self.ident, cd.ap[:, h * 128:(h + 1) * 128], False, True,
                            cd.bufs, [self.psb[sb]], inc=(r == 3), skip_group_check=True)
                    if r < 3:
                        piece(kp[bs][m], 128 * (r + 1), 512, False, True)

            def vt_fn(m, j):
                return (vh.ap[:, j * 129:(j + 1) * 129], [vh.bufs[j]])

            def done(c, m, accs):
                sb_ = stg.bufs
                odr = odd[c % 2]
                odc = odr.ap[:, 0:512].rearrange("p (a b) -> p a b", b=128)
                odb = odr.bufs
                sm2 = sms[c % 2]
                od3 = odc
                for (q0, n_) in ((0, 3), (3, 1)):
                    ab = accs[q0][0]
                    i0_ = m * 4 + q0
                    T.op("dve", lambda e, ab=ab, i0_=i0_, n_=n_: e.tensor_copy(out=stg.ap[:, i0_ * 129:(i0_ + n_) * 129], in_=self.PS(ab, 0, n_ * 129)),
                         reads=[self.psb[ab]], writes=sb_)
                if m == 0:
                    return
                T.op("dve", lambda e: e.reciprocal(sm.ap[:, 0:8], a3[:, :, 128]), reads=sb_, writes=sm.bufs)
                T.op("dve", lambda e: e.tensor_scalar(sm.ap[:, 4:8], sm.ap[:, 4:8], self.neglam[:, 0:1], None, ALU.mult), reads=sm.bufs, writes=sm.bufs)

                def bc(lo, hi):
                    return sm.ap[:, lo:hi].unsqueeze(2).to_broadcast([128, hi - lo, 128])
                o0 = a3[:, 0:4, 0:128]
                o1 = a3[:, 4:8, 0:128]
                T.op("dve", lambda e: e.tensor_tensor(out=od3, in0=o0, in1=bc(0, 4), op=ALU.mult), reads=sb_ + sm.bufs, writes=odb)
                T.op("dve", lambda e: e.tensor_tensor(out=o1, in0=o1, in1=bc(4, 8), op=ALU.mult), reads=sb_ + sm.bufs, writes=sb_)
                T.op("dve", lambda e: e.tensor_tensor(out=od3, in0=od3, in1=o1, op=ALU.add), reads=sb_ + odb, writes=odb)
                T.op("dve", lambda e: e.tensor_tensor(out=o1, in0=od3, in1=od3, op=ALU.mult), reads=odb, writes=sb_)
                T.op("dve", lambda e: e.tensor_reduce(out=sm2.ap[:, 8:12], in_=o1, op=ALU.add, axis=mybir.AxisListType.X), reads=sb_, writes=sm2.bufs)

                def part_b():
                    T.op("act", lambda e: e.activation(out=sm2.ap[:, 8:12], in_=sm2.ap[:, 8:12], func=AF.Ln, scale=1.0 / 128, bias=self.epsb),
                         reads=sm2.bufs, writes=sm2.bufs)
                    T.op("act", lambda e: e.activation(out=sm2.ap[:, 8:12], in_=sm2.ap[:, 8:12], func=AF.Exp, scale=-0.5), reads=sm2.bufs, writes=sm2.bufs)
                    rsb = sm2.ap[:, 8:12].unsqueeze(2).to_broadcast([128, 4, 128])
                    T.op("dve", lambda e: e.tensor_tensor(out=odc, in0=odc, in1=rsb, op=ALU.mult), reads=odb + sm2.bufs, writes=odb)
                    ov = ost.ap[:, c * 512:(c + 1) * 512].rearrange("p (a b) -> p a b", b=128)
                    gb = gsub.unsqueeze(1).to_broadcast([128, 4, 128])
                    T.op("dve", lambda e, ov=ov: e.tensor_tensor(out=ov, in0=odc, in1=gb, op=ALU.mult), reads=odb, writes=[ost.bufs[c * 4 + q_] for q_ in range(4)])
                return part_b

            def keep_fn(c, j):
                if j < 4 * c:
                    dmin = 512 * c - (128 * j + 127)
                elif j > 4 * c + 3:
                    dmin = 128 * j - (512 * c + 511)
                else:
                    return True
                return slope * dmin < 104.0
            self.attention_core(2, 64, 128, qk_fn, vt_fn, 1.0, lambda m, j, c, sb: None, done, keep_fn=keep_fn, side_units=side)
            hg = h % 2
            self.transpose_out(ost, lambda tc: otr.ap[:, hg * S + tc * 512: hg * S + (tc + 1) * 512], lambda tc: [otr.bufs[hg * 4 + tc]])

        def outproj(g2):
            for n in range(NCH):
                wo, wob = self.wget(("wo1h", g2, n))
                q4 = self.bank_ctr.get("quad", 0)
                self.bank_ctr["quad"] = q4 + 1
                b0 = (q4 % 2) * 4
                for tc in range(4):
                    b = b0 + tc
                    for k2 in range(2):
                        self.mm(self.PS(b), wo[:, k2 * 128:(k2 + 1) * 128], otr.ap[:, k2 * S + tc * 512: k2 * S + (tc + 1) * 512], k2 == 0, k2 == 1,
                                [wob, otr.bufs[k2 * 4 + tc]], [self.psb[b]])
                hh = self.H.ap[:, n * S:(n + 1) * S]
                pw_ = self.ps[:, b0 * 512:(b0 + 4) * 512]
                T.op("dve", lambda e, hh=hh, pw_=pw_: e.tensor_tensor(out=hh, in0=hh, in1=pw_, op=ALU.add),
                     reads=[self.psb[b0 + i_] for i_ in range(4)], writes=[self.Hb[n][i_] for i_ in range(4)])

        self.bank_groups.update({"s": [0, 1, 2], "acc0": [3, 4], "acc1": [5, 6], "side": [7]})
        for u in proj(0):
            u()
        for h in range(8):
            side = proj(h + 1) if h + 1 < 8 else None
            attn(h, side)
            if h % 2 == 1:
                outproj(h // 2)
        for r_ in [otr, cd, ost, stg, sm, self.pt] + odd + sms + vhs + qa[0] + qa[1] + kp[0] + kp[1] + kn[0] + kn[1]:
            r_.free()

    def emit_all(self):
        T = self.T
        for s in range(self.nseq):
            self.bank_ctr = {}
            for tc in range(4):
                for c in range(NCH):
                    src = self.xT[s, c][:, tc * 512:(tc + 1) * 512]
                    dst = self.H.ap[:, c * S + tc * 512: c * S + (tc + 1) * 512]
                    T.dma("sp", lambda e, dst=dst, src=src: e.dma_start(out=dst, in_=src), writes=[self.Hb[c][tc]])
            stages = [("l0", self.layer0), ("f0", lambda: self.ffn(0)), ("l1", self.layer1), ("f1", lambda: self.ffn(1))]
            for name, fn in stages:
                fn()
                if self.stop_after == name:
                    break
            ob = self.arena.alloc(4 * 2048, F32, 4, "outstage")
            self.octr = 0
            if self.stop_after is None:
                self.final_norm(ob, s)
            else:
                for kc in range(NCH):
                    for tc in range(4):
                        src = self.Hap(kc, tc)
                        dst = self.outT[s, kc][:, tc * 512:(tc + 1) * 512]
                        t = T.dma("sp", lambda e, dst=dst, src=src: e.dma_start(out=dst, in_=src), reads=[self.Hb[kc][tc]])
                        self.out_tickets.append(t)
            ob.free()
        for t in self.out_tickets:
            T.wait_ticket("sp", t)

    def final_norm(self, ob, s):
        T = self.T
        A = self.arena
        sq = A.alloc(2 * 1024, BF16, 2, "sqf")
        rs = A.alloc(2 * 2048, F32, 2, "rstdf")
        g = self.cst("final_g")
        for tc in range(4):
            b = self.bank()
            for kc in range(NCH):
                sqa = sq.ap[:, (kc % 2) * 512:(kc % 2 + 1) * 512]
                src = self.Hap(kc, tc)
                T.op("act", lambda e, sqa=sqa, src=src: e.activation(out=sqa, in_=src, func=AF.Square),
                     reads=[self.Hb[kc][tc]], writes=[sq.bufs[kc % 2]])
                self.mm(self.PS(b), self.ones, sqa, kc == 0, kc == NCH - 1, [sq.bufs[kc % 2]], [self.psb[b]], inc=True)
            r = rs.ap[:, (tc % 2) * 512:(tc % 2 + 1) * 512]
            rb = rs.bufs[tc % 2]
            T.op("act", lambda e, r=r, b=b: e.activation(out=r, in_=self.PS(b), func=AF.Ln, scale=1.0 / D, bias=self.epsb),
                 reads=[self.psb[b]], writes=[rb])
            T.op("act", lambda e, r=r: e.activation(out=r, in_=r, func=AF.Exp, scale=-0.5), reads=[rb], writes=[rb])
            for kc in range(NCH):
                k = (tc * NCH + kc) % 4
                o = ob.ap[:, k * 512:(k + 1) * 512]
                src = self.Hap(kc, tc)
                T.op("dve", lambda e, o=o, src=src, kc=kc, r=r: e.scalar_tensor_tensor(
                    out=o, in0=src, scalar=g[:, kc:kc + 1], in1=r, op0=ALU.mult, op1=ALU.mult),
                    reads=[self.Hb[kc][tc], rb], writes=[ob.bufs[k]])
                dst = self.outT[s, kc][:, tc * 512:(tc + 1) * 512]
                t = T.dma("sp", lambda e, dst=dst, o=o: e.dma_start(out=dst, in_=o), reads=[ob.bufs[k]])
                self.out_tickets.append(t)
        sq.free()
        rs.free()

    def build(self):
        nc = bass.Bass("TRN2", target_bir_lowering=False)
        self.nc = nc
        self.xT = nc.dram_tensor("xT", [self.nseq, NCH, 128, S], F32, kind="ExternalInput").ap()
        self.wflat = nc.dram_tensor("wflat", [128, self.wtot], F32, kind="ExternalInput").ap()
        self.cflat = nc.dram_tensor("cflat", [128, self.ctot], F32, kind="ExternalInput").ap()
        self.outT = nc.dram_tensor("outT", [self.nseq, NCH, 128, S], F32, kind="ExternalOutput").ap()
        npers = self.clay["kb"][0] + 256
        with ExitStack() as st:
            arena_t = st.enter_context(nc.sbuf_tensor("arena", [128, ARENA_KB * 512], BF16))
            self.ps = st.enter_context(nc.psum_tensor("ps", [128, 8 * 512], F32))
            T = Tracker()
            self.T = T
            for name in ("pe", "act", "dve", "pool", "sp"):
                T.eng[name] = Eng(name, st.enter_context(nc.semaphore("sem_" + name)))
            for q in ("sp", "pool"):
                for i in range(12):
                    T.dsems[q].append(DSem(st.enter_context(nc.semaphore("dsem_%s%d" % (q, i)))))
            block = st.enter_context(nc.Block())

            for dry in (True, False):
                T.dry = dry
                self.arena = Arena(arena_t, ARENA_KB * 1024)
                A = self.arena
                self.psb = [Buf("ps%d" % i) for i in range(8)]
                self.bank_groups = {"g": list(range(8)), "s": [0, 1, 2, 3], "acc": [4, 5, 6]}
                self.bank_ctr = {}
                self.out_tickets = []
                self.wuse = 0
                self.wissued = 0
                cp = A.alloc(npers * 4, F32, 1, "cpers")
                self.cpers = cp.ap
                misc = A.alloc(128 * 2 * 2 + 64 + 128 * 4 + 64 * 4 * 2 + 64, BF16, 1, "misc")
                self.ones = misc.ap[:, 0:128]
                self.ident = misc.ap[:, 128:256]
                mf = misc.ap[:, 256:].bitcast(F32)
                self.epsb = mf[:, 0:1]
                self.neglam = mf[:, 1:2]
                lamt = mf[:, 2:8]
                self.gsub = mf[:, 16:144]
                ltmp = mf[:, 144:272]
                self.wring = A.alloc(NSLOT * SLOT * 2, BF16, NSLOT, "wring")
                self.H = A.alloc(NCH * S * 4, F32, 1, "H")
                self.Hb = [[Buf("H%d_%d" % (c, t)) for t in range(4)] for c in range(NCH)]
                self.HN = A.alloc(NCH * S * 2, BF16, 1, "HN")
                self.HNb = [[Buf("HN%d_%d" % (c, t)) for t in range(4)] for c in range(NCH)]
                T.dma("sp", lambda e: e.dma_start(out=self.cpers, in_=self.cflat[:, 0:npers]), writes=cp.bufs)
                T.op("dve", lambda e: e.memset(self.ones, 1.0), writes=misc.bufs)
                T.op("dve", lambda e: e.memset(self.epsb, EPS), writes=misc.bufs)
                T.op("dve", lambda e: e.tensor_copy(out=self.ident, in_=self.cst("ident")), reads=cp.bufs, writes=misc.bufs)
                lam_init = 0.8 - 0.6 * math.exp(-0.3 * 1)
                T.op("dve", lambda e: e.tensor_scalar(self.gsub, self.cst("subln_g"), 1.0 - lam_init, None, ALU.mult), reads=cp.bufs, writes=misc.bufs)
                for i, (a, b_) in enumerate((("diff_lambda_q1", "diff_lambda_k1"), ("diff_lambda_q2", "diff_lambda_k2"))):
                    T.op("dve", lambda e, a=a, b_=b_, i=i: e.tensor_tensor(out=ltmp[:, i * 64:(i + 1) * 64], in0=self.cst(a), in1=self.cst(b_), op=ALU.mult),
                         reads=cp.bufs, writes=misc.bufs)
                    T.op("dve", lambda e, i=i: e.reduce_sum(lamt[:, i:i + 1], ltmp[:, i * 64:(i + 1) * 64], axis=mybir.AxisListType.X),
                         reads=misc.bufs, writes=misc.bufs)
                T.op("act", lambda e: e.activation(out=lamt[:, 0:2], in_=lamt[:, 0:2], func=AF.Exp), reads=misc.bufs, writes=misc.bufs)
                T.op("dve", lambda e: e.scalar_tensor_tensor(out=self.neglam, in0=lamt[:, 1:2], scalar=-lam_init, in1=lamt[:, 0:1], op0=ALU.add, op1=ALU.subtract),
                     reads=misc.bufs, writes=misc.bufs)
                self.emit_all()

            progs = {n: T.eng[n].prog for n in T.eng}
            sems = {n: T.eng[n].sem for n in T.eng}

            def replay(name):
                def run(e):
                    sem = sems[name]
                    for item in progs[name]:
                        if item[0] == "wait":
                            e.wait_ge(item[1], item[2])
                        elif item[0] == "op":
                            ins = item[1](e)
                            if item[2]:
                                ins.then_inc(sem, 1)
                        else:
                            item[1](e).then_inc(item[2], 16)
                return run

            block.tensor(replay("pe"))
            block.scalar(replay("act"))
            block.vector(replay("dve"))
            block.gpsimd(replay("pool"))
            block.sync(replay("sp"))
        return nc


_CACHE = {}


def _get_program(nseq, stop_after=None):
    key = (nseq, stop_after)
    if key not in _CACHE:
        p = Program(nseq, stop_after)
        p.build()
        _CACHE[key] = p
    return _CACHE[key]


def kernel(**inputs):
    inp = {k: np.asarray(v) for k, v in inputs.items()}
    x = inp["x"]
    B = x.shape[0]
    ncores = int(os.environ.get("KNCORES", NCORES))
    stop_after = os.environ.get("KSTOP") or None
    nseq = B // ncores
    prog = _get_program(nseq, stop_after)
    _, _, wflat = weight_catalog(inp)
    _, _, cflat = const_catalog(inp)
    in_maps = []
    for c in range(ncores):
        xs = x[c * nseq:(c + 1) * nseq]
        xT = np.ascontiguousarray(xs.transpose(0, 2, 1)).reshape(nseq, NCH, 128, S)
        in_maps.append({"xT": xT, "wflat": wflat, "cflat": cflat})
    res = run_bass_kernel_spmd(prog.nc, in_maps, core_ids=list(range(ncores)))
    outs = []
    for c in range(ncores):
        oT = np.asarray(res.results[c]["outT"]).reshape(nseq, D, S)
        outs.append(oT.transpose(0, 2, 1))
    return np.ascontiguousarray(np.concatenate(outs, axis=0)).astype(np.float32, copy=False)
```
